# Optimizing a Trainium2 kernel written in Bass

```python
import math
import jax, jax.numpy as jnp
from jax import lax
import numpy as np

D_MODEL = 1024
BATCH = 4
SEQ = 8192
DEPTH = 1

D_A = D_MODEL
D_B = D_MODEL
D_MIX = D_A + D_B
N_GROUPS_A = 8
N_HEADS_B = 8
HEAD_DIM_B = D_B // N_HEADS_B
CHUNK = 128
SHORT_CONV = 3
FILTER_EMB = 33
FILTER_HIDDEN = 64
DECAY_TARGET = 1e-2
FAST_DECAY_PCT = 0.3
SLOW_DECAY_PCT = 1.5
D_IN = 4 * D_A + 3 * D_B
EPS = 1e-6

kernel_name = "hyena_gmlp_parallel_hybrid_block"


def rms_norm(x, w):
    xf = x.astype(jnp.float32)
    y = xf * lax.rsqrt(jnp.mean(xf * xf, axis=-1, keepdims=True) + EPS)
    return (y * w.astype(jnp.float32)).astype(x.dtype)


def short_conv_centred(u, w, b):
    up = jnp.pad(u, ((0, 0), (1, 1), (0, 0)))
    return up[:, :-2] * w[0] + up[:, 1:-1] * w[1] + up[:, 2:] * w[2] + b


def hyena_pos_features(L):
    bands = (FILTER_EMB - 1) // 2
    t = jnp.linspace(0.0, 1.0, L, dtype=jnp.float32)[:, None]
    w = 2.0 * math.pi * jnp.arange(L, dtype=jnp.float32)[:, None] / L
    f = jnp.linspace(1e-4, bands - 1, bands, dtype=jnp.float32)[None, :]
    return jnp.concatenate([t, jnp.cos(f * w), -jnp.sin(f * w)], axis=-1), t


def hyena_filter(L, w1, b1, fr1, w2, b2, fr2, w3, b3, fr3, w_o):
    f32 = jnp.float32
    z, t = hyena_pos_features(L)
    h = jnp.sin(fr1.astype(f32) * (z @ w1.astype(f32) + b1.astype(f32)))
    h = jnp.sin(fr2.astype(f32) * (h @ w2.astype(f32) + b2.astype(f32)))
    h = jnp.sin(fr3.astype(f32) * (h @ w3.astype(f32) + b3.astype(f32)))
    h = h @ w_o.astype(f32)
    deltas = jnp.abs(jnp.linspace(math.log(DECAY_TARGET) / SLOW_DECAY_PCT,
                                  math.log(DECAY_TARGET) / FAST_DECAY_PCT,
                                  D_A, dtype=f32))
    decay = jnp.exp(-t * deltas[None, :])
    h_fwd = h[:, :D_A] * decay
    h_bwd = h[:, D_A:] * decay
    l1 = jnp.sum(jnp.abs(h_fwd), axis=0) + jnp.sum(jnp.abs(h_bwd[1:]), axis=0)
    k = jnp.concatenate([h_fwd, jnp.zeros((1, D_A), f32), h_bwd[1:][::-1]], axis=0)
    return k / (l1[None, :] + EPS)


def long_conv_bidir(u, k, skip):
    L = u.shape[1]
    uf = u.astype(jnp.float32)
    U = jnp.fft.rfft(uf, n=2 * L, axis=1)
    K = jnp.fft.rfft(k, axis=0)
    y = jnp.fft.irfft(U * K[None], n=2 * L, axis=1)[:, :L]
    return (y + uf * skip.astype(jnp.float32)).astype(u.dtype)


def setup_inputs(seed: int = 0) -> dict:
    key = jax.random.key(seed)
    ks = jax.random.split(key, 24)
    n = jax.random.normal
    f32 = jnp.float32
    return {
        "x": n(ks[0], (BATCH, SEQ, D_MODEL), f32),
        "pre_norm_w": 1.0 + 0.05 * n(ks[1], (D_MODEL,), f32),
        "w_in": n(ks[2], (D_MODEL, D_IN), f32) * D_MODEL ** -0.5,
        "conv_w": n(ks[3], (SHORT_CONV, 3 * D_A), f32) * SHORT_CONV ** -0.5,
        "conv_b": 0.01 * n(ks[4], (3 * D_A,), f32),
        "filt_w1": n(ks[5], (FILTER_EMB, FILTER_HIDDEN), f32) * FILTER_EMB ** -0.5,
        "filt_b1": 0.1 * n(ks[6], (FILTER_HIDDEN,), f32),
        "filt_freq1": 1.0 + 0.05 * n(ks[7], (FILTER_HIDDEN,), f32),
        "filt_w2": n(ks[8], (FILTER_HIDDEN, FILTER_HIDDEN), f32) * FILTER_HIDDEN ** -0.5,
        "filt_b2": 0.1 * n(ks[9], (FILTER_HIDDEN,), f32),
        "filt_freq2": 1.0 + 0.05 * n(ks[10], (FILTER_HIDDEN,), f32),
        "filt_w3": n(ks[11], (FILTER_HIDDEN, FILTER_HIDDEN), f32) * FILTER_HIDDEN ** -0.5,
        "filt_b3": 0.1 * n(ks[12], (FILTER_HIDDEN,), f32),
        "filt_freq3": 1.0 + 0.05 * n(ks[13], (FILTER_HIDDEN,), f32),
        "filt_w_out": n(ks[14], (FILTER_HIDDEN, 2 * D_A), f32) * FILTER_HIDDEN ** -0.5,
        "hyena_skip": 0.1 * n(ks[15], (D_A,), f32),
        "sgu_norm_w": 1.0 + 0.05 * n(ks[16], (D_B,), f32),
        "sgu_norm_b": 0.01 * n(ks[17], (D_B,), f32),
        "sgu_w": n(ks[18], (N_HEADS_B, CHUNK, CHUNK), f32) * CHUNK ** -0.5,
        "sgu_b": 1.0 + 0.05 * n(ks[19], (N_HEADS_B, CHUNK), f32),
        "w_out": n(ks[20], (D_MIX, D_MODEL), f32) * D_MIX ** -0.5,
        "post_norm_w": 1.0 + 0.05 * n(ks[21], (D_MODEL,), f32),
    }


def reference(x, pre_norm_w, w_in, conv_w, conv_b, filt_w1, filt_b1, filt_freq1,
              filt_w2, filt_b2, filt_freq2, filt_w3, filt_b3, filt_freq3, filt_w_out,
              hyena_skip, sgu_norm_w, sgu_norm_b, sgu_w, sgu_b, w_out, post_norm_w):
    B, L, _ = x.shape
    for _layer in range(DEPTH):
        h = rms_norm(x, pre_norm_w)
        proj = h @ w_in
        hy_in, hy_gate, sg_u, sg_v, sg_gate = jnp.split(
            proj, [3 * D_A, 4 * D_A, 4 * D_A + D_B, 4 * D_A + 2 * D_B], axis=-1)

        hy_in = short_conv_centred(hy_in, conv_w, conv_b)
        x0, x1, v = jnp.split(hy_in, 3, axis=-1)
        k = hyena_filter(L, filt_w1, filt_b1, filt_freq1, filt_w2, filt_b2, filt_freq2,
                         filt_w3, filt_b3, filt_freq3, filt_w_out)
        y_a = x0 * long_conv_bidir(v * x1, k, hyena_skip)
        y_a = y_a * jax.nn.silu(hy_gate)

        vf = sg_v.astype(jnp.float32).reshape(B, L, N_HEADS_B, HEAD_DIM_B)
        mu = jnp.mean(vf, axis=-1, keepdims=True)
        var = jnp.mean(jnp.square(vf - mu), axis=-1, keepdims=True)
        vn = ((vf - mu) * lax.rsqrt(var + EPS)
              * sgu_norm_w.astype(jnp.float32).reshape(N_HEADS_B, HEAD_DIM_B)
              + sgu_norm_b.astype(jnp.float32).reshape(N_HEADS_B, HEAD_DIM_B)).astype(x.dtype)
        vc = vn.reshape(B, L // CHUNK, CHUNK, N_HEADS_B, HEAD_DIM_B)
        mixed = jnp.einsum('hpq,bnqhd->bnphd', sgu_w, vc) \
            + jnp.transpose(sgu_b)[None, None, :, :, None]
        y_b = sg_u * mixed.reshape(B, L, D_B) * jax.nn.silu(sg_gate)

        y = jnp.concatenate([y_a, y_b], axis=-1) @ w_out
        x = x + rms_norm(y, post_norm_w)
    return x
```

```python
import math
import os
import contextlib
import numpy as np
import concourse.bass as bass
import concourse.mybir as mybir
from concourse.bass_utils import run_bass_kernel_spmd

F32 = mybir.dt.float32
BF16 = mybir.dt.bfloat16
AF = mybir.ActivationFunctionType
ALU = mybir.AluOpType
AX = mybir.AxisListType

D = 1024
L = 8192
NT = 4096
NF = 16384
EPS = 1e-6
NDQ = 12


class Tok:
    __slots__ = ("key", "val", "eng")

    def __init__(self, key, val, eng):
        self.key, self.val, self.eng = key, val, eng


class Buf:
    def __init__(self):
        self.lw = None
        self.rd = {}


class Plan:
    def __init__(self, nc):
        self.nc = nc
        self.names = ["pe", "act", "dve", "pool", "sp"]
        self.streams = {n: [] for n in self.names}
        self.cnt = {n: 0 for n in self.names}
        self.waited = {n: {} for n in self.names}
        self.stack = contextlib.ExitStack()
        self.semh = {n: self.stack.enter_context(nc.semaphore("s_" + n)) for n in self.names}
        self.dq = {}
        for q in ("sp", "pool"):
            for k in range(NDQ):
                key = "dq_%s_%d" % (q, k)
                self.semh[key] = self.stack.enter_context(nc.semaphore(key))
                self.cnt[key] = 0
            self.dq[q] = 0

    def _deps(self, stream, r, w, skip_same):
        deps = []
        for b in r:
            if b.lw is not None:
                deps.append(b.lw)
        for b in w:
            deps.extend(b.rd.values())
            if b.lw is not None:
                deps.append(b.lw)
        waits = []
        wd = self.waited[stream]
        for t in deps:
            if skip_same and t.eng == stream:
                continue
            if wd.get(t.key, 0) >= t.val:
                continue
            wd[t.key] = t.val
            waits.append((t.key, t.val))
        return waits

    def op(self, stream, fn, r=(), w=()):
        waits = self._deps(stream, r, w, stream == "pe")
        self.cnt[stream] += 1
        tok = Tok(stream, self.cnt[stream], stream)
        self.streams[stream].append((waits, fn, (stream, 1)))
        for b in r:
            b.rd[stream] = tok
        for b in w:
            b.lw = tok
            b.rd = {}
        return tok

    def dma(self, q, out, in_, r=(), w=()):
        waits = self._deps(q, r, w, False)
        k = self.dq[q]
        self.dq[q] = (k + 1) % NDQ
        key = "dq_%s_%d" % (q, k)
        self.cnt[key] += 16
        tok = Tok(key, self.cnt[key], None)
        self.streams[q].append((waits, lambda e, o=out, i=in_: e.dma_start(out=o, in_=i), (key, 16)))
        for b in r:
            b.rd[key] = tok
        for b in w:
            b.lw = tok
            b.rd = {}
        return tok

    def barrier(self):
        toks = [(n, self.cnt[n]) for n in self.names if self.cnt[n] > 0]
        for q in ("sp", "pool"):
            for k in range(NDQ):
                key = "dq_%s_%d" % (q, k)
                if self.cnt[key] > 0:
                    toks.append((key, self.cnt[key]))
        for n in self.names:
            waits = []
            for key, val in toks:
                if self.waited[n].get(key, 0) >= val:
                    continue
                self.waited[n][key] = val
                waits.append((key, val))
            self.streams[n].append((waits, None, None))

    def final_waits(self, stream):
        waits = []
        for q in ("sp", "pool"):
            for k in range(NDQ):
                key = "dq_%s_%d" % (q, k)
                if self.cnt[key] > 0:
                    waits.append((key, self.cnt[key]))
        self.streams[stream].append((waits, None, None))

    def emit(self, stream, e):
        for waits, fn, inc in self.streams[stream]:
            for key, val in waits:
                e.wait_ge(self.semh[key], val)
            if fn is None:
                continue
            ins = fn(e)
            ins.then_inc(self.semh[inc[0]], inc[1])


class Arena:
    def __init__(self, t, dtype, n):
        self.t, self.dtype, self.n = t, dtype, n
        self.base = 0
        self.off = 0

    def mark(self):
        self.base = self.off

    def reset(self):
        self.off = self.base

    def alloc(self, *shape, parts=128):
        n = int(np.prod(shape))
        n4 = (n + 15) // 16 * 16
        assert self.off + n4 <= self.n, ("arena overflow", self.off, n4, self.n)
        ap = self.t[0:parts, self.off:self.off + n]
        self.off += n4
        if len(shape) == 2:
            ap = ap.rearrange("p (a b) -> p a b", a=shape[0])
        elif len(shape) == 3:
            ap = ap.rearrange("p (a b c) -> p a b c", a=shape[0], b=shape[1])
        return ap


def build_program():
    nc = bass.Bass("TRN2", target_bir_lowering=False)

    def din(name, shape, dt=F32):
        return nc.dram_tensor(name, list(shape), dt, kind="ExternalInput")

    xT_o = din("xT_o", [128, 8, 4098])
    xT_s = din("xT_s", [128, 8, 4098])
    xn = din("xn", [32, 128, 1024])
    win = din("win", [56, 128, 8, 128])
    wout = din("wout", [16, 128, 1024])
    prew = din("prew", [128, 8])
    cw = din("cw", [128, 72])
    cb = din("cb", [128, 24])
    skip = din("skip", [128, 8])
    lnw = din("lnw", [128, 8])
    lnb = din("lnb", [128, 8])
    sgub = din("sgub", [8, 128])
    sguwT = din("sguwT", [8, 128, 128])
    postw = din("postw", [1, 1024])
    fw1 = din("fw1", [33, 64])
    fw2 = din("fw2", [64, 64])
    fw3 = din("fw3", [64, 64])
    fwo = din("fwo", [64, 2048])
    ffb = din("ffb", [64, 6])
    zT = din("zT", [33, NF])
    tl = din("tl", [1, NF])
    ndelta = din("ndelta", [128, 8])
    tabs_b = din("tabs_b", [128, 3072], BF16)
    tabs_f = din("tabs_f", [128, 1552])
    out = nc.dram_tensor("out", [32, 128, 1024], F32, kind="ExternalOutput")

    KDEBUG = os.environ.get("KDEBUG", "0") == "1"
    skind = "ExternalOutput" if KDEBUG else "Internal"
    ubuf = nc.dram_tensor("ubuf", [1024, L], BF16, kind=skind)
    wsp = nc.dram_tensor("wsp", [1024, NT], BF16, kind=skind)
    w2sp = nc.dram_tensor("w2sp", [1024, NT], BF16, kind=skind)
    ybsp = nc.dram_tensor("ybsp", [1024, NT], BF16, kind=skind)
    kkd = nc.dram_tensor("kkd", [1024, NF], BF16, kind=skind)
    ycd = nc.dram_tensor("ycd", [1024, NT], BF16, kind=skind)

    NB = 61440
    NFL = 22400
    tb = nc.alloc_sbuf_tensor("arena_b", [128, NB], BF16)
    tf = nc.alloc_sbuf_tensor("arena_f", [128, NFL], F32)
    AB = Arena(tb, BF16, NB)
    AFL = Arena(tf, F32, NFL)
    psum = [nc.alloc_psum_tensor("ps%d" % i, [128, 512], F32) for i in range(8)]
    PB = [Buf() for _ in range(8)]

    P = Plan(nc)
    op, dma = P.op, P.dma
    dbgn = [0]

    def dump(name, ap, buf, dt=F32):
        if not KDEBUG:
            return
        shp = list(ap.shape)
        dtn = nc.dram_tensor("dbg_" + name, shp, dt, kind="ExternalOutput")
        dma("pool", dtn.ap(), ap, r=[buf], w=[Buf()])

    def mm(o, l, r_, st, sp_):
        return lambda e: e.matmul(o, l, r_, start=st, stop=sp_)

    TB = AB.alloc(3072)
    TF = AFL.alloc(1552)
    b_tabs = Buf()
    dma("sp", TB, tabs_b[:, :], w=[b_tabs])
    dma("sp", TF, tabs_f[:, :], w=[b_tabs])
    ones = TB[:, 0:128]
    F1a = TB[:, 128:384]
    Wr = TB[:, 384:512]
    Wi = TB[:, 512:640]
    nWi = TB[:, 640:768]
    G1a = TB[:, 768:1024]
    G1b = TB[:, 1024:1280]
    Vr4 = TB[:, 1280:1312]
    nVi4 = TB[:, 1312:1344]
    F1d = TB[:, 1408:1664]
    F1a65 = TB[:, 1664:1794]
    F1d65 = TB[:, 1794:1924]
    Vr4w = TB[:, 1924:1956]
    nVi4w = TB[:, 1956:1988]
    TTb_P = TB[:, 2048:2560]
    TTb_n = TB[:, 2560:2816]
    TTb_p = TB[:, 2816:3072]
    TT_P = TF[:, 0:512]
    TT_n = TF[:, 512:768]
    TT_p = TF[:, 768:1024]
    TT_P65 = TF[:, 1024:1284]
    TT_n65 = TF[:, 1284:1414]
    TT_p65 = TF[:, 1414:1544]

    vec = AFL.alloc(256)
    b_vec = Buf()
    prew_s = vec[:, 0:8]
    cw_s = vec[:, 8:80]
    cb_s = vec[:, 80:104]
    skip_s = vec[:, 104:112]
    lnw_s = vec[:, 112:120]
    lnb_s = vec[:, 120:128]
    ndel_s = vec[:, 128:136]
    inv_s = vec[:, 136:144]
    eps_s = vec[:, 144:145]
    mpi_s = vec[:, 145:146]
    ffb_s = vec[:, 146:152]
    frb_s = vec[:, 152:155]
    zero_s = vec[:, 155:156]
    for dst, src in ((prew_s, prew), (cw_s, cw), (cb_s, cb), (skip_s, skip), (lnw_s, lnw),
                     (lnb_s, lnb), (ndel_s, ndelta)):
        dma("sp", dst, src[:, :], w=[b_vec])
    dma("sp", ffb_s[0:64, :], ffb[:, :], w=[b_vec])
    op("dve", lambda e: e.memset(eps_s, EPS), w=[b_vec])
    op("dve", lambda e: e.memset(mpi_s, -math.pi), w=[b_vec])
    op("dve", lambda e: e.memset(zero_s, 0.0), w=[b_vec])
    for k in range(3):
        op("dve", lambda e, k=k: e.tensor_tensor(frb_s[0:64, k:k + 1], ffb_s[0:64, 2 * k:2 * k + 1],
                                                 ffb_s[0:64, 2 * k + 1:2 * k + 2], ALU.mult),
           r=[b_vec], w=[b_vec])
    l1acc = AFL.alloc(8, 32)
    b_l1 = Buf()
    op("dve", lambda e: e.memset(l1acc, 0.0), w=[b_l1])
    AB.mark()
    AFL.mark()

    hT = AB.alloc(8, 4098)
    b_hT = Buf()
    xsts = [AFL.alloc(8, 512) for _ in range(2)]
    b_xsts = [Buf(), Buf()]
    sq = AB.alloc(8, 512)
    b_sq = Buf()
    rstd = AFL.alloc(512)
    b_rstd = Buf()
    wst1 = AFL.alloc(8, 512)
    wst = [wst1, wst1]
    b_wst1 = Buf()
    b_wst = [b_wst1, b_wst1]
    wbf = [AB.alloc(4, 8, 128) for _ in range(2)]
    b_wbf = [Buf() for _ in range(2)]
    NSL = 2
    Pf = [AB.alloc(2, 514) for _ in range(NSL)]
    b_Pf = [[Buf(), Buf()] for _ in range(NSL)]
    cacc = [AFL.alloc(2, 512) for _ in range(NSL)]
    b_cacc = [[Buf(), Buf()] for _ in range(NSL)]
    sgs = [AB.alloc(512) for _ in range(NSL)]
    b_sgs = [Buf() for _ in range(NSL)]
    uts = [AB.alloc(512) for _ in range(NSL)]
    b_uts = [Buf() for _ in range(NSL)]
    wts = [AB.alloc(512) for _ in range(NSL)]
    b_wts = [Buf() for _ in range(NSL)]
    w2ts = [AB.alloc(512) for _ in range(NSL)]
    b_w2ts = [Buf() for _ in range(NSL)]
    b_hal = [PB[4], PB[5]]
    b_ubuf, b_wsp, b_w2sp, b_ybsp, b_kkd, b_ycd = Buf(), Buf(), Buf(), Buf(), Buf(), Buf()
    wcount = [0]
    ucount = [0]

    def preprocess(xT_d):
        chunks = [(512 * k, 512) for k in range(8)] + [(4096, 2)]
        for ci, (c0, n) in enumerate(chunks):
            xst, b_xst = xsts[ci % 2], b_xsts[ci % 2]
            dma("sp", xst[:, :, 0:n], xT_d[:, :, c0:c0 + n], w=[b_xst])
            op("act", lambda e, n=n, xst=xst: e.activation(sq[:, :, 0:n], xst[:, :, 0:n], AF.Square),
               r=[b_xst], w=[b_sq])
            for kt in range(8):
                op("pe", mm(psum[7][:, 0:n], ones, sq[:, kt, 0:n], kt == 0, kt == 7),
                   r=[b_sq, b_tabs], w=[PB[7]])
            op("act", lambda e, n=n: e.activation(rstd[:, 0:n], psum[7][:, 0:n], AF.Sqrt,
                                                  bias=eps_s, scale=1.0 / D),
               r=[PB[7], b_vec], w=[b_rstd])
            op("dve", lambda e, n=n: e.reciprocal(rstd[:, 0:n], rstd[:, 0:n]), r=[b_rstd], w=[b_rstd])
            for kt in range(8):
                op("dve", lambda e, n=n, kt=kt, c0=c0, xst=xst: e.tensor_tensor(
                    hT[:, kt, c0:c0 + n], xst[:, kt, 0:n], rstd[:, 0:n], ALU.mult),
                   r=[b_xst, b_rstd], w=[b_hT])

    def load_w(cts):
        ws = wcount[0] % 2
        wcount[0] += 1
        for i, ct in enumerate(cts):
            dma("sp", wst[ws][:, :, 128 * i:128 * i + 128], win[ct, :, :, :], w=[b_wst[ws]])
        n = len(cts)
        for kt in range(8):
            op("act", lambda e, kt=kt, n=n, ws=ws: e.activation(
                wbf[ws][:, 0:n, kt, :], wst[ws][:, kt, 0:128 * n].rearrange("p (a b) -> p a b", a=n),
                AF.Copy, scale=prew_s[:, kt:kt + 1]),
               r=[b_wst[ws], b_vec], w=[b_wbf[ws]])
        return ws

    def proj_fm(bank, ws, wi, c0):
        for kt in range(8):
            op("pe", mm(psum[bank][:, :], wbf[ws][:, wi, kt, :], hT[:, kt, c0:c0 + 512], kt == 0, kt == 7),
               r=[b_wbf[ws], b_hT], w=[PB[bank]])

    def halo_proj(ws, wi, sl, off, j):
        for kt in range(8):
            op("pe", mm(psum[4 + sl][:, off:off + 2], wbf[ws][:, wi, kt, :],
                        hT[:, kt, 512 * j:512 * j + 514:513], kt == 0, kt == 7),
               r=[b_wbf[ws], b_hT], w=[b_hal[sl]])

    def evac_pf(sl, k, bank, off):
        op("act", lambda e: e.activation(Pf[sl][:, k, 1:513], psum[bank][:, :], AF.Copy),
           r=[PB[bank]], w=[b_Pf[sl][k]])
        op("act", lambda e: e.activation(Pf[sl][:, k, 0:514:513], psum[4 + sl][:, off:off + 2],
                                         AF.Copy), r=[b_hal[sl]], w=[b_Pf[sl][k]])

    def unitA(ws, g, j, tbase, sl):
        bA, bB = 2 * sl, 2 * sl + 1
        c0 = 1 + 512 * j
        proj_fm(bA, ws, 0, c0)
        proj_fm(bB, ws, 1, c0)
        halo_proj(ws, 0, sl, 0, j)
        halo_proj(ws, 1, sl, 2, j)
        evac_pf(sl, 0, bA, 0)
        evac_pf(sl, 1, bB, 2)
        cts = (8 + g, 16 + g)
        for tap in range(3):
            for k in range(2):
                ct = cts[k]
                acc = cacc[sl][:, k, :]
                if tap == 0:
                    bank = (bA, bB)[k]
                    op("act", lambda e, acc=acc, bank=bank, ct=ct: e.activation(
                        acc, psum[bank][:, :], AF.Identity, bias=cb_s[:, ct:ct + 1],
                        scale=cw_s[:, 3 * ct + 1:3 * ct + 2]), r=[PB[bank], b_vec], w=[b_cacc[sl][k]])
                else:
                    lo = 0 if tap == 1 else 2
                    wi_ = 3 * ct + (0 if tap == 1 else 2)
                    op("dve", lambda e, acc=acc, k=k, lo=lo, wi_=wi_: e.scalar_tensor_tensor(
                        acc, Pf[sl][:, k, lo:lo + 512], cw_s[:, wi_:wi_ + 1], acc, ALU.mult, ALU.add),
                       r=[b_Pf[sl][k], b_vec], w=[b_cacc[sl][k]])
        op("dve", lambda e: e.tensor_tensor(uts[sl], cacc[sl][:, 0, :], cacc[sl][:, 1, :], ALU.mult),
           r=[b_cacc[sl][0], b_cacc[sl][1]], w=[b_uts[sl]])
        dma("pool", ubuf[128 * g:128 * g + 128, tbase + 512 * j:tbase + 512 * j + 512], uts[sl],
            r=[b_uts[sl]], w=[b_ubuf])

    def unitB(ws, g, j, sl, slA):
        bA, bB = 2 * sl, 2 * sl + 1
        c0 = 1 + 512 * j
        proj_fm(bA, ws, 2, c0)
        proj_fm(bB, ws, 3, c0)
        halo_proj(ws, 2, sl, 0, j)
        evac_pf(sl, 0, bA, 0)
        op("act", lambda e: e.activation(sgs[sl], psum[bB][:, :], AF.Silu), r=[PB[bB]], w=[b_sgs[sl]])
        ct = g
        acc = cacc[sl][:, 0, :]
        pf = Pf[sl]
        op("act", lambda e: e.activation(acc, psum[bA][:, :], AF.Identity, bias=cb_s[:, ct:ct + 1],
                                         scale=cw_s[:, 3 * ct + 1:3 * ct + 2]),
           r=[PB[bA], b_vec], w=[b_cacc[sl][0]])
        for lo, wi_ in ((0, 3 * ct), (2, 3 * ct + 2)):
            op("dve", lambda e, lo=lo, wi_=wi_: e.scalar_tensor_tensor(
                acc, pf[:, 0, lo:lo + 512], cw_s[:, wi_:wi_ + 1], acc, ALU.mult, ALU.add),
               r=[b_Pf[sl][0], b_vec], w=[b_cacc[sl][0]])
        op("dve", lambda e: e.tensor_tensor(wts[sl], acc, sgs[sl], ALU.mult),
           r=[b_cacc[sl][0], b_sgs[sl]], w=[b_wts[sl]])
        op("dve", lambda e: e.scalar_tensor_tensor(w2ts[sl], uts[slA], skip_s[:, g:g + 1], wts[sl],
                                                   ALU.mult, ALU.mult),
           r=[b_uts[slA], b_wts[sl], b_vec], w=[b_w2ts[sl]])
        cs_ = slice(512 * j, 512 * j + 512)
        dma("pool", wsp[128 * g:128 * g + 128, cs_], wts[sl], r=[b_wts[sl]], w=[b_wsp])
        dma("pool", w2sp[128 * g:128 * g + 128, cs_], w2ts[sl], r=[b_w2ts[sl]], w=[b_w2sp])

    def hyena_pass(own):
        tbase = 0 if own else NT

        def cts_of(g):
            return [8 + g, 16 + g, g, 24 + g] if own else [8 + g, 16 + g]
        nxt = load_w(cts_of(0))
        for g in range(8):
            ws = nxt
            if g + 1 < 8:
                nxt = load_w(cts_of(g + 1))
            for j in range(8):
                sl = ucount[0] % NSL
                ucount[0] += 1
                unitA(ws, g, j, tbase, sl)
                if own:
                    sl2 = ucount[0] % NSL
                    ucount[0] += 1
                    unitB(ws, g, j, sl2, sl)

    def two(fn):
        return [fn() for _ in range(2)]
    vsb = two(lambda: AFL.alloc(512))
    b_vsb = two(Buf)
    vsq = two(lambda: AFL.alloc(512))
    b_vsq = two(Buf)
    st = two(lambda: AFL.alloc(16))
    b_st = two(Buf)
    nrm = two(lambda: AB.alloc(512))
    b_nrm = [[Buf() for _ in range(4)] for _ in range(2)]
    mixed = two(lambda: AFL.alloc(512))
    b_mixed = [[Buf() for _ in range(4)] for _ in range(2)]
    tmpf = two(lambda: AFL.alloc(512))
    b_tmpf = two(Buf)
    ybt = two(lambda: AB.alloc(512))
    b_ybt = two(Buf)
    sgq = two(lambda: AB.alloc(512))
    b_sgq = two(Buf)
    wsT = two(lambda: AB.alloc(128))
    b_wsT = two(Buf)
    wsTf = two(lambda: AFL.alloc(128))
    b_wsTf = two(Buf)
    sgb_bc = two(lambda: AFL.alloc(128))
    b_sgb = two(Buf)
    C2 = two(lambda: AFL.alloc(128))
    b_C2 = two(Buf)
    gcount = [0]

    def gmlp_head_prep(hd):
        hs = hd % 2
        dma("sp", wsTf[hs], sguwT[hd, :, :], w=[b_wsTf[hs]])
        op("act", lambda e: e.activation(wsT[hs], wsTf[hs], AF.Copy), r=[b_wsTf[hs]], w=[b_wsT[hs]])
        dma("sp", sgb_bc[hs], sgub[hd:hd + 1, :].partition_broadcast(128).rearrange("p o n -> p (o n)"),
            w=[b_sgb[hs]])

    def gmlp_tile(ws, hd, j, sl):
        hs = hd % 2
        bU, bG, bV, bM = 4 * sl, 4 * sl + 1, 4 * sl + 2, 4 * sl + 3
        c0 = 1 + 512 * j
        S = st[sl]
        bS = b_st[sl]

        def s0():
            for q in range(4):
                for kt in range(8):
                    op("pe", mm(psum[bV][:, 128 * q:128 * q + 128],
                                hT[:, kt, c0 + 128 * q:c0 + 128 * q + 128], wbf[ws][:, 1, kt, :],
                                kt == 0, kt == 7),
                       r=[b_wbf[ws], b_hT], w=[PB[bV]])
            proj_fm(bU, ws, 0, c0)
            proj_fm(bG, ws, 2, c0)

        def s1():
            op("act", lambda e: e.activation(vsb[sl], psum[bV][:, :], AF.Copy), r=[PB[bV]], w=[b_vsb[sl]])
            op("act", lambda e: e.activation(vsq[sl], psum[bV][:, :], AF.Square), r=[PB[bV]], w=[b_vsq[sl]])
            op("act", lambda e: e.activation(sgq[sl], psum[bG][:, :], AF.Silu), r=[PB[bG]], w=[b_sgq[sl]])

        def s2():
            op("dve", lambda e: e.tensor_reduce(S[:, 0:4], vsb[sl].rearrange("p (a b) -> p a b", a=4),
                                                AX.X, ALU.add), r=[b_vsb[sl]], w=[bS])

        def s3():
            op("dve", lambda e: e.tensor_reduce(S[:, 4:8], vsq[sl].rearrange("p (a b) -> p a b", a=4),
                                                AX.X, ALU.add), r=[b_vsq[sl]], w=[bS])

        def s4():
            op("dve", lambda e: e.tensor_scalar(S[:, 8:12], S[:, 0:4], 1.0 / 128, None, ALU.mult),
               r=[bS], w=[bS])

        def s5():
            op("dve", lambda e: e.tensor_tensor(S[:, 12:16], S[:, 8:12], S[:, 8:12], ALU.mult), r=[bS], w=[bS])

        def s6():
            op("dve", lambda e: e.tensor_scalar(S[:, 4:8], S[:, 4:8], 1.0 / 128, None, ALU.mult),
               r=[bS], w=[bS])

        def s7():
            op("dve", lambda e: e.tensor_tensor(S[:, 12:16], S[:, 4:8], S[:, 12:16], ALU.subtract),
               r=[bS], w=[bS])

        def s8():
            op("act", lambda e: e.activation(S[:, 12:16], S[:, 12:16], AF.Sqrt, bias=eps_s, scale=1.0),
               r=[bS, b_vec], w=[bS])

        def s9():
            op("dve", lambda e: e.reciprocal(S[:, 12:16], S[:, 12:16]), r=[bS], w=[bS])

        def s10():
            for q in range(4):
                op("dve", lambda e, q=q: e.tensor_scalar(
                    nrm[sl][:, 128 * q:128 * q + 128], vsb[sl][:, 128 * q:128 * q + 128],
                    S[:, 8 + q:9 + q], S[:, 12 + q:13 + q], ALU.subtract, ALU.mult),
                   r=[b_vsb[sl], bS], w=[b_nrm[sl][q]])

        def s11():
            for q in range(4):
                op("pe", mm(psum[bM][:, 128 * q:128 * q + 128], nrm[sl][:, 128 * q:128 * q + 128], wsT[hs],
                            True, True), r=[b_nrm[sl][q], b_wsT[hs]], w=[PB[bM]])

        def s12():
            for q in range(4):
                op("dve", lambda e, q=q: e.scalar_tensor_tensor(
                    mixed[sl][:, 128 * q:128 * q + 128], psum[bM][:, 128 * q:128 * q + 128],
                    lnw_s[:, hd:hd + 1], C2[hs], ALU.mult, ALU.add),
                   r=[PB[bM], b_C2[hs], b_vec], w=[b_mixed[sl][q]])

        def s13():
            op("dve", lambda e: e.tensor_tensor(tmpf[sl], psum[bU][:, :], mixed[sl], ALU.mult),
               r=[PB[bU]] + b_mixed[sl], w=[b_tmpf[sl]])

        def s14():
            op("dve", lambda e: e.tensor_tensor(ybt[sl], tmpf[sl], sgq[sl], ALU.mult),
               r=[b_tmpf[sl], b_sgq[sl]], w=[b_ybt[sl]])
            dma("pool", ybsp[128 * hd:128 * hd + 128, 512 * j:512 * j + 512], ybt[sl],
                r=[b_ybt[sl]], w=[b_ybsp])

        return [s0, s1, s2, s3, s4, s5, s6, s7, s8, s9, s10, s11, s12, s13, s14]

    def gmlp_pass():
        nxt = load_w([32, 40, 48])
        gmlp_head_prep(0)
        for hd in range(8):
            ws = nxt
            hs = hd % 2
            op("pe", mm(psum[7][:, 0:128], ones, wsT[hs], True, True), r=[b_wsT[hs], b_tabs], w=[PB[7]])
            op("dve", lambda e, hd=hd, hs=hs: e.scalar_tensor_tensor(
                C2[hs], psum[7][:, 0:128], lnb_s[:, hd:hd + 1], sgb_bc[hs], ALU.mult, ALU.add),
               r=[PB[7], b_sgb[hs], b_vec], w=[b_C2[hs]])
            if hd + 1 < 8:
                nxt = load_w([32 + hd + 1, 40 + hd + 1, 48 + hd + 1])
                gmlp_head_prep(hd + 1)
            for jp in range(4):
                sa = gmlp_tile(ws, hd, 2 * jp, 0)
                sb = gmlp_tile(ws, hd, 2 * jp + 1, 1)
                for k in range(len(sa)):
                    sa[k]()
                    sb[k]()

    KSTOP = int(os.environ.get("KSTOP", "9"))

    def finish():
        P.final_waits("pool")
        with nc.Block() as block:
            @block.tensor
            def _(e):
                P.emit("pe", e)

            @block.scalar
            def _(e):
                P.emit("act", e)

            @block.vector
            def _(e):
                P.emit("dve", e)

            @block.gpsimd
            def _(e):
                P.emit("pool", e)

            @block.sync
            def _(e):
                P.emit("sp", e)
        P.stack.close()
        return nc

    preprocess(xT_o)
    if KSTOP == 0:
        return finish()
    hyena_pass(False)
    if KSTOP == 1:
        return finish()
    preprocess(xT_s)
    hyena_pass(True)
    P.barrier()
    gmlp_pass()
    if KSTOP == 2:
        return finish()
    P.barrier()
    AB.reset()
    AFL.reset()

    w1s = AFL.alloc(64)
    w2s = AFL.alloc(64)
    w3s = AFL.alloc(64)
    wos = AFL.alloc(2048)
    b_fw = Buf()
    dma("sp", w1s[0:33, :], fw1[:, :], w=[b_fw])
    dma("sp", w2s[0:64, :], fw2[:, :], w=[b_fw])
    dma("sp", w3s[0:64, :], fw3[:, :], w=[b_fw])
    dma("sp", wos[0:64, :], fwo[:, :], w=[b_fw])
    wos_b = AB.alloc(2048)
    b_wosb = Buf()
    op("act", lambda e: e.activation(wos_b[0:64, :], wos[0:64, :], AF.Copy), r=[b_fw], w=[b_wosb])
    h3b = AB.alloc(2048)
    zb = AFL.alloc(2048)
    b_zb = Buf()
    tlb = AFL.alloc(2048)
    b_tlb = Buf()
    hbuf = [AFL.alloc(2048) for _ in range(3)]
    b_h = [Buf() for _ in range(3)]
    argbs = [AFL.alloc(512) for _ in range(2)]
    b_args = [Buf(), Buf()]
    argks = [AFL.alloc(512) for _ in range(2)]
    b_argks = [Buf(), Buf()]
    decbs = [AFL.alloc(512) for _ in range(2)]
    b_decs = [Buf(), Buf()]
    kkss = [AB.alloc(2048) for _ in range(2)]
    b_kkss = [[Buf() for _ in range(4)] for _ in range(2)]
    junks = [AFL.alloc(512) for _ in range(2)]
    b_junks = [Buf(), Buf()]
    b_l1list = []

    def b_l1x():
        b_l1list.append(Buf())
        return b_l1list[-1]
    MAGIC = 12582912.0
    mlp_banks = [0, 2, 3]
    out_banks = [4, 5, 6, 7]
    mcount = 0
    ocount = 0
    kcount = 0
    for blk in range(8):
        cs = slice(2048 * blk, 2048 * blk + 2048)
        dma("sp", zb[0:33, :], zT[:, cs], w=[b_zb])
        dma("sp", tlb, tl[0:1, cs].partition_broadcast(128).rearrange("p o n -> p (o n)"), w=[b_tlb])
        srcs = [(zb, b_zb, 33, w1s), (hbuf[0], b_h[0], 64, w2s), (hbuf[1], b_h[1], 64, w3s)]
        for li, (src, bsrc, kk_, wl) in enumerate(srcs):
            for c4 in range(4):
                c_ = slice(512 * c4, 512 * c4 + 512)
                bk = mlp_banks[mcount % 3]
                argb, b_arg = argbs[mcount % 2], b_args[mcount % 2]
                argk, b_argk = argks[mcount % 2], b_argks[mcount % 2]
                mcount += 1
                op("pe", mm(psum[bk][0:64, :], wl[0:kk_, :], src[0:kk_, c_], True, True),
                   r=[b_fw, bsrc], w=[PB[bk]])
                op("act", lambda e, li=li, bk=bk, argb=argb: e.activation(
                    argb[0:64, :], psum[bk][0:64, :], AF.Identity, bias=frb_s[0:64, li:li + 1],
                    scale=ffb_s[0:64, 2 * li:2 * li + 1]), r=[PB[bk], b_vec], w=[b_arg])
                op("dve", lambda e, argb=argb, argk=argk: e.tensor_scalar(
                    argk[0:64, :], argb[0:64, :], 1.0 / (2 * math.pi), MAGIC, ALU.mult, ALU.add),
                   r=[b_arg], w=[b_argk])
                op("dve", lambda e, argk=argk: e.tensor_scalar(argk[0:64, :], argk[0:64, :], -MAGIC, 2 * math.pi,
                                                              ALU.add, ALU.mult), r=[b_argk], w=[b_argk])
                op("dve", lambda e, argb=argb, argk=argk: e.tensor_tensor(
                    argb[0:64, :], argb[0:64, :], argk[0:64, :], ALU.subtract), r=[b_arg, b_argk], w=[b_arg])
                hdst = h3b if li == 2 else hbuf[li]
                op("act", lambda e, c_=c_, argb=argb, hdst=hdst: e.activation(
                    hdst[0:64, c_], argb[0:64, :], AF.Sin, bias=zero_s[0:64, :], scale=1.0),
                   r=[b_arg, b_vec], w=[b_h[li]])
        for ct in range(8):
            wcol = (0 if blk < 4 else 1024) + 128 * ct
            kks, b_kks = kkss[kcount % 2], b_kkss[kcount % 2]
            kcount += 1
            for c4 in range(4):
                c_ = slice(512 * c4, 512 * c4 + 512)
                bk = out_banks[ocount % 4]
                decb, b_dec = decbs[ocount % 2], b_decs[ocount % 2]
                junk, b_junk = junks[ocount % 2], b_junks[ocount % 2]
                ocount += 1
                op("pe", mm(psum[bk][:, :], wos_b[0:64, wcol:wcol + 128], h3b[0:64, c_], True, True),
                   r=[b_wosb, b_h[2]], w=[PB[bk]])
                op("act", lambda e, ct=ct, c_=c_, decb=decb: e.activation(decb, tlb[:, c_], AF.Exp,
                                                                         scale=ndel_s[:, ct:ct + 1]),
                   r=[b_tlb, b_vec], w=[b_dec])
                op("dve", lambda e, c_=c_, bk=bk, decb=decb, kks=kks: e.tensor_tensor(
                    kks[:, c_], psum[bk][:, :], decb, ALU.mult), r=[PB[bk], b_dec], w=[b_kks[c4]])
                op("dve", lambda e, ct=ct, blk=blk, c4=c4, c_=c_, kks=kks: e.tensor_reduce(
                    l1acc[:, ct, 4 * blk + c4:4 * blk + c4 + 1], kks[:, c_], AX.X, ALU.add,
                    apply_absolute_value=True),
                   r=[b_kks[c4], b_l1], w=[b_l1x()])
            dma("pool", kkd[128 * ct:128 * ct + 128, cs], kks, r=b_kks, w=[b_kkd])
    op("dve", lambda e: e.tensor_reduce(inv_s, l1acc, AX.X, ALU.add), r=[b_l1] + b_l1list, w=[b_vec])
    op("dve", lambda e: e.tensor_scalar(inv_s, inv_s, EPS, float(NF), ALU.add, ALU.mult), r=[b_vec], w=[b_vec])
    op("dve", lambda e: e.reciprocal(inv_s, inv_s), r=[b_vec], w=[b_vec])
    dump("inv", inv_s, b_vec)
    dump("l1acc", l1acc, b_l1)
    if KSTOP == 3:
        return finish()
    P.barrier()
    AB.reset()
    AFL.reset()

    wo_b = AB.alloc(16, 1024)
    b_wo = Buf()
    wo_st = AFL.alloc(1024)
    b_wost = Buf()
    for ct in range(16):
        dma("sp", wo_st, wout[ct, :, :], w=[b_wost])
        op("act", lambda e, ct=ct: e.activation(wo_b[:, ct, :], wo_st, AF.Copy), r=[b_wost], w=[b_wo])
    AB.mark()
    AFL.mark()
    KKs = [AB.alloc(16, 128) for _ in range(2)]
    b_KKs = [Buf(), Buf()]
    Zs = [AB.alloc(16, 128) for _ in range(2)]
    b_Zs = [Buf(), Buf()]
    Y16s = [AB.alloc(16, 128) for _ in range(2)]
    b_Y16s = [Buf(), Buf()]
    PQs = [AB.alloc(2, 2, 130) for _ in range(4)]
    b_PQs = [[Buf(), Buf(), Buf()] for _ in range(4)]
    PQ2s = [AB.alloc(4, 2, 65) for _ in range(4)]
    b_PQ2s = [[Buf(), Buf(), Buf()] for _ in range(4)]
    PQ3s = [AB.alloc(2, 2, 256) for _ in range(4)]
    b_PQ3s = [[Buf(), Buf(), Buf()] for _ in range(4)]
    Kcs = [AB.alloc(528) for _ in range(4)]
    b_Kcs = [[Buf(), Buf(), Buf()] for _ in range(4)]
    Cbs = [AB.alloc(512) for _ in range(4)]
    b_Cbs = [Buf() for _ in range(4)]
    kkv = kkd.ap().rearrange("c (a p) -> a c p", p=128)
    ubv = ubuf.ap().rearrange("c (a p) -> a c p", p=128)
    ycv = ycd.ap().rearrange("c (a p) -> a c p", p=128)

    def twiddle_f(sl, dst, bdst):
        a = psum[2 * sl][:, 0:260]
        pb = PB[2 * sl]
        a3 = a.rearrange("p (c r k) -> p c r k", c=2, r=2)
        d0 = dst[:, 0, :, :].rearrange("p c k -> p (c k)")
        d1 = dst[:, 1, :, :].rearrange("p c (r k) -> p c r k", r=2)
        op("dve", lambda e: e.tensor_tensor(d0, a, TT_P65, ALU.mult), r=[pb, b_tabs], w=[bdst[0]])
        op("dve", lambda e: e.tensor_tensor(d1[:, :, 0, :], a3[:, :, 1, :],
                                            TT_n65.rearrange("p (c k) -> p c k", c=2), ALU.mult),
           r=[pb, b_tabs], w=[bdst[1]])
        op("dve", lambda e: e.tensor_tensor(d1[:, :, 1, :], a3[:, :, 0, :],
                                            TT_p65.rearrange("p (c k) -> p c k", c=2), ALU.mult),
           r=[pb, b_tabs], w=[bdst[2]])

    def twiddle_i(sl, dst, bdst):
        a = Cbs[sl][0:65, :]
        pb = b_Cbs[sl]
        op("act", lambda e: e.activation(a, psum[2 * sl + 1][0:65, :], AF.Copy), r=[PB[2 * sl + 1]], w=[pb])
        a3 = a.rearrange("p (c r k) -> p c r k", c=2, r=2)
        d0 = dst[0:65, 0, :, :].rearrange("p c k -> p (c k)")
        d1 = dst[0:65, 1, :, :].rearrange("p c (r k) -> p c r k", r=2)
        op("dve", lambda e: e.tensor_tensor(d0, a, TTb_P[0:65, :], ALU.mult), r=[pb, b_tabs], w=[bdst[0]])
        op("dve", lambda e: e.tensor_tensor(d1[:, :, 0, :], a3[:, :, 1, :],
                                            TTb_p[0:65, :].rearrange("p (c k) -> p c k", c=2), ALU.mult),
           r=[pb, b_tabs], w=[bdst[1]])
        op("dve", lambda e: e.tensor_tensor(d1[:, :, 1, :], a3[:, :, 0, :],
                                            TTb_n[0:65, :].rearrange("p (c k) -> p c k", c=2), ALU.mult),
           r=[pb, b_tabs], w=[bdst[2]])

    def stage2b(sl):
        src, bsrc = PQs[sl], b_PQs[sl]
        X = psum[2 * sl + 1]

        def part(pq, r):
            return src[:, pq, :, 65 * r:65 * r + 65]
        seq_r = [(Wr, part(0, 0)), (Wr, part(1, 0)), (nWi, part(0, 1)), (nWi, part(1, 1))]
        seq_i = [(Wi, part(0, 0)), (Wi, part(1, 0)), (Wr, part(0, 1)), (Wr, part(1, 1))]
        for half, seq in ((0, seq_r), (1, seq_i)):
            o = X[:, 130 * half:130 * half + 130].rearrange("p (c k) -> p c k", c=2)
            for i, (wm, rr) in enumerate(seq):
                op("pe", mm(o, wm, rr, i == 0, i == 3), r=bsrc + [b_tabs], w=[PB[2 * sl + 1]])

    def chunk_stages(gs, sl, cc0):
        KK, Z, Y16 = KKs[gs], Zs[gs], Y16s[gs]
        A = psum[2 * sl]
        X = psum[2 * sl + 1]
        C = X
        Yp = A
        pA, pX = PB[2 * sl], PB[2 * sl + 1]
        pC, pY = pX, pA
        Kc, bKc = Kcs[sl], b_Kcs[sl]
        PQ2, bPQ2 = PQ2s[sl], b_PQ2s[sl]
        PQ3, bPQ3 = PQ3s[sl], b_PQ3s[sl]

        def s1f():
            for c in range(2):
                op("pe", mm(A[:, 130 * c:130 * c + 130], KK[:, cc0 + c, :], F1a65, True, True),
                   r=[b_KKs[gs], b_tabs], w=[pA])

        def twf():
            twiddle_f(sl, PQs[sl], b_PQs[sl])

        def s2():
            stage2b(sl)

        def evk():
            op("act", lambda e: e.activation(Kc[:, 0:130], X[:, 0:130], AF.Copy), r=[pX], w=[bKc[0]])
            op("act", lambda e: e.activation(Kc[:, 130:390], X[:, 0:260], AF.Copy), r=[pX], w=[bKc[1]])
            op("act", lambda e: e.activation(Kc[:, 390:520], X[:, 130:260], AF.Copy, scale=-1.0),
               r=[pX], w=[bKc[2]])

        def s1d():
            for c in range(2):
                op("pe", mm(A[:, 130 * c:130 * c + 130], Z[0:64, cc0 + c, :], F1d65[0:64, :], True, True),
                   r=[b_Zs[gs], b_tabs], w=[pA])

        def kmul():
            fl = PQ2.rearrange("p a c k -> p (a c k)")
            op("dve", lambda e: e.tensor_tensor(fl[:, 0:260], X[:, 0:260], Kc[:, 0:260], ALU.mult),
               r=[pX, bKc[0], bKc[1]], w=[bPQ2[0]])
            op("dve", lambda e: e.tensor_tensor(fl[:, 260:390], X[:, 130:260], Kc[:, 390:520], ALU.mult),
               r=[pX, bKc[2]], w=[bPQ2[1]])
            op("dve", lambda e: e.tensor_tensor(fl[:, 390:520], X[:, 0:130], Kc[:, 260:390], ALU.mult),
               r=[pX, bKc[1]], w=[bPQ2[2]])

        def s3():
            for c in range(2):
                o = C[0:65, 256 * c:256 * c + 256]
                seq = [(0, G1a), (2, G1a), (1, G1b), (3, G1b)]
                for i, (pi_, tab) in enumerate(seq):
                    op("pe", mm(o, PQ2[:, pi_, c, :], tab, i == 0, i == 3), r=bPQ2 + [b_tabs], w=[pC])

        def itw():
            twiddle_i(sl, PQ3, bPQ3)

        def s4():
            def part3(pq, r):
                return PQ3[0:65, pq, :, 128 * r:128 * r + 128]
            seq = [(Vr4w[0:65, :], part3(0, 0)), (Vr4w[0:65, :], part3(1, 0)),
                   (nVi4w[0:65, :], part3(0, 1)), (nVi4w[0:65, :], part3(1, 1))]
            for i, (wm, rr) in enumerate(seq):
                op("pe", mm(Yp[0:32, 0:256].rearrange("p (c k) -> p c k", c=2), wm, rr, i == 0, i == 3),
                   r=bPQ3 + [b_tabs], w=[pY])

        def evy():
            op("act", lambda e: e.activation(Y16[0:32, cc0:cc0 + 2, :].rearrange("p c k -> p (c k)"),
                                             Yp[0:32, 0:256], AF.Copy), r=[pY], w=[b_Y16s[gs]])

        def s2_s1d():
            s2()
            s1d()

        return [s1f, twf, s2_s1d, evk, twf, s2, kmul, s3, itw, s4, evy]

    for grp in range(64):
        gs = grp % 2
        ch0 = 16 * grp
        dma("sp", KKs[gs], kkv[:, ch0:ch0 + 16, :], r=[b_kkd], w=[b_KKs[gs]])
        dma("sp", Zs[gs][0:64, :, :], ubv[:, ch0:ch0 + 16, :], r=[b_ubuf], w=[b_Zs[gs]])
        for quad in range(2):
            sts = [chunk_stages(gs, sl, 8 * quad + 2 * sl) for sl in range(4)]
            for k in range(len(sts[0])):
                for sl in range(4):
                    sts[sl][k]()
        dma("pool", ycv[:, ch0:ch0 + 16, :], Y16s[gs][0:32, :, :], r=[b_Y16s[gs]], w=[b_ycd])
    if KSTOP == 4:
        return finish()
    P.barrier()
    AB.reset()
    AFL.reset()

    pw_bc = AFL.alloc(1024)
    b_pw = Buf()
    dma("sp", pw_bc, postw[0:1, :].partition_broadcast(128).rearrange("p o n -> p (o n)"), w=[b_pw])
    yc_t = two(lambda: AB.alloc(8, 512))
    b_yct = two(Buf)
    w_t = two(lambda: AB.alloc(8, 512))
    b_w_t = two(Buf)
    w2_t = two(lambda: AB.alloc(8, 512))
    b_w2_t = two(Buf)
    ya_t = two(lambda: AB.alloc(8, 512))
    b_ya = [[Buf() for _ in range(8)] for _ in range(2)]
    yb_t = two(lambda: AB.alloc(8, 512))
    b_yb = two(Buf)
    N3 = 3
    xt_ = [AFL.alloc(1024) for _ in range(N3)]
    b_xt = [Buf() for _ in range(N3)]
    ot = [AFL.alloc(1024) for _ in range(N3)]
    b_ot = [[Buf(), Buf()] for _ in range(N3)]
    sq2 = [AFL.alloc(1024) for _ in range(N3)]
    b_sq2 = [Buf() for _ in range(N3)]
    ss = [AFL.alloc(4) for _ in range(N3)]
    b_ss = [Buf() for _ in range(N3)]

    def v3(d):
        return d.ap().rearrange("(g c) t -> c g t", c=128)

    scount = 0
    for j in range(8):
        jp = j % 2
        ts_ = slice(512 * j, 512 * j + 512)
        dma("sp", yc_t[jp], v3(ycd)[:, :, ts_], r=[b_ycd], w=[b_yct[jp]])
        dma("sp", w_t[jp], v3(wsp)[:, :, ts_], r=[b_wsp], w=[b_w_t[jp]])
        dma("sp", w2_t[jp], v3(w2sp)[:, :, ts_], r=[b_w2sp], w=[b_w2_t[jp]])
        dma("sp", yb_t[jp], v3(ybsp)[:, :, ts_], r=[b_ybsp], w=[b_yb[jp]])
        for g in range(8):
            op("dve", lambda e, g=g, jp=jp: e.scalar_tensor_tensor(
                ya_t[jp][:, g, :], yc_t[jp][:, g, :], inv_s[:, g:g + 1], w_t[jp][:, g, :], ALU.mult, ALU.mult),
               r=[b_yct[jp], b_w_t[jp], b_vec], w=[b_ya[jp][g]])
        for g in range(8):
            op("dve", lambda e, g=g, jp=jp: e.tensor_tensor(ya_t[jp][:, g, :], ya_t[jp][:, g, :],
                                                           w2_t[jp][:, g, :], ALU.add),
               r=[b_w2_t[jp]], w=[b_ya[jp][g]])
        for q in range(4):
            tix = 4 * j + q
            k3 = scount % N3
            scount += 1
            b0 = 2 * k3
            dma("sp", xt_[k3], xn[tix, :, :], w=[b_xt[k3]])
            for hf in range(2):
                pbk = PB[b0 + hf]
                for ct in range(16):
                    if ct < 8:
                        src, bsrc = ya_t[jp][:, ct, 128 * q:128 * q + 128], b_ya[jp][ct]
                    else:
                        src, bsrc = yb_t[jp][:, ct - 8, 128 * q:128 * q + 128], b_yb[jp]
                    op("pe", mm(psum[b0 + hf][:, :], src, wo_b[:, ct, 512 * hf:512 * hf + 512], ct == 0, ct == 15),
                       r=[bsrc, b_wo], w=[pbk])
            op("dve", lambda e, k3=k3: e.memset(ss[k3], 0.0), w=[b_ss[k3]])
            for hf in range(2):
                op("act", lambda e, hf=hf, k3=k3, b0=b0: e.activation(
                    sq2[k3][:, 512 * hf:512 * hf + 512], psum[b0 + hf][:, :],
                    AF.Square, accum_out=ss[k3][:, hf:hf + 1]),
                   r=[PB[b0 + hf]], w=[b_sq2[k3], b_ss[k3]])
            op("dve", lambda e, k3=k3: e.tensor_tensor(ss[k3][:, 2:3], ss[k3][:, 0:1], ss[k3][:, 1:2], ALU.add),
               r=[b_ss[k3]], w=[b_ss[k3]])
            op("act", lambda e, k3=k3: e.activation(ss[k3][:, 3:4], ss[k3][:, 2:3], AF.Sqrt, bias=eps_s,
                                                    scale=1.0 / D), r=[b_ss[k3], b_vec], w=[b_ss[k3]])
            op("dve", lambda e, k3=k3: e.reciprocal(ss[k3][:, 3:4], ss[k3][:, 3:4]), r=[b_ss[k3]], w=[b_ss[k3]])
            for hf in range(2):
                op("dve", lambda e, hf=hf, k3=k3, b0=b0: e.scalar_tensor_tensor(
                    ot[k3][:, 512 * hf:512 * hf + 512], psum[b0 + hf][:, :], ss[k3][:, 3:4],
                    pw_bc[:, 512 * hf:512 * hf + 512], ALU.mult, ALU.mult),
                   r=[PB[b0 + hf], b_ss[k3], b_pw], w=[b_ot[k3][hf]])
            for hf in range(2):
                op("dve", lambda e, hf=hf, k3=k3: e.tensor_tensor(
                    ot[k3][:, 512 * hf:512 * hf + 512], ot[k3][:, 512 * hf:512 * hf + 512],
                    xt_[k3][:, 512 * hf:512 * hf + 512], ALU.add),
                   r=[b_xt[k3]], w=[b_ot[k3][hf]])
            dma("pool", out[tix, :, :], ot[k3], r=b_ot[k3], w=[Buf()])
    P.final_waits("pool")

    with nc.Block() as block:
        @block.tensor
        def _(e):
            P.emit("pe", e)

        @block.scalar
        def _(e):
            P.emit("act", e)

        @block.vector
        def _(e):
            P.emit("dve", e)

        @block.gpsimd
        def _(e):
            P.emit("pool", e)

        @block.sync
        def _(e):
            P.emit("sp", e)
    P.stack.close()
    return nc


def _tables(h):
    n = np.arange(128)
    ang = 2 * np.pi * np.outer(n, n) / 128.0
    wr, wi = np.cos(ang), -np.sin(ang)
    tb = np.zeros((128, 3072), np.float64)
    tb[:, 0:128] = 1.0
    tb[:, 128:256], tb[:, 256:384] = wr, wi
    tb[:, 384:512], tb[:, 512:640], tb[:, 640:768] = wr, wi, -wi
    tb[:, 768:896], tb[:, 896:1024] = wr, -wi
    tb[:, 1024:1152], tb[:, 1152:1280] = wi, wr
    own = np.arange(32 * h, 32 * h + 32)
    tb[:, 1280:1312] = wr[:, own]
    tb[:, 1312:1344] = wi[:, own]
    perm = np.array([(32 * h + a) if a < 32 else (32 * (1 - h) + a - 32) for a in range(64)])
    tb[0:64, 1408:1536] = wr[perm, :]
    tb[0:64, 1536:1664] = wi[perm, :]
    tb[:, 1664:1729], tb[:, 1729:1794] = wr[:, 0:65], wi[:, 0:65]
    tb[0:64, 1794:1859], tb[0:64, 1859:1924] = wr[perm, 0:65], wi[perm, 0:65]
    wgt = np.full((65, 1), 2.0)
    wgt[0, 0] = 1.0
    wgt[64, 0] = 1.0
    tb[0:65, 1924:1956] = wgt * wr[0:65][:, own]
    tb[0:65, 1956:1988] = wgt * wi[0:65][:, own]
    ang2 = 2 * np.pi * np.outer(n, n) / float(NF)
    tr, ti = np.cos(ang2), -np.sin(ang2)
    t65, i65 = tr[:, 0:65], ti[:, 0:65]
    tb[:, 2048:3072] = np.concatenate([tr, tr, tr, tr, -ti, -ti, ti, ti], axis=1)
    tf = np.concatenate([tr, tr, tr, tr, -ti, -ti, ti, ti, t65, t65, t65, t65, -i65, -i65, i65, i65,
                         np.zeros((128, 8))], axis=1)
    import ml_dtypes
    return tb.astype(np.float32).astype(ml_dtypes.bfloat16), tf.astype(np.float32)


def _consts():
    bands = 16
    t = np.linspace(0.0, 1.0, L, dtype=np.float32)[:, None]
    w = (2.0 * np.float32(math.pi) * np.arange(L, dtype=np.float32)[:, None] / np.float32(L)).astype(np.float32)
    f = np.linspace(1e-4, bands - 1, bands, dtype=np.float32)[None, :]
    z = np.concatenate([t, np.cos(f * w), -np.sin(f * w)], axis=-1).astype(np.float32)
    order = np.concatenate([np.arange(L), [0], np.arange(L - 1, 0, -1)])
    zT = np.ascontiguousarray(z[order].T)
    tl = t[order, 0].copy()
    tl[L] = 1.0e4
    deltas = np.abs(np.linspace(math.log(1e-2) / 1.5, math.log(1e-2) / 0.3, D, dtype=np.float32))
    ndelta = np.ascontiguousarray((-deltas).reshape(8, 128).T)
    return zT.astype(np.float32), tl.reshape(1, NF).astype(np.float32), ndelta.astype(np.float32)


def kernel(x, pre_norm_w, w_in, conv_w, conv_b, filt_w1, filt_b1, filt_freq1, filt_w2, filt_b2,
           filt_freq2, filt_w3, filt_b3, filt_freq3, filt_w_out, hyena_skip, sgu_norm_w, sgu_norm_b,
           sgu_w, sgu_b, w_out, post_norm_w):
    f = lambda a: np.ascontiguousarray(np.asarray(a, dtype=np.float32))
    x = f(x)
    nc = build_program()
    zT, tl, ndelta = _consts()
    pk = lambda v: f(np.asarray(v).reshape(-1, 128).T)
    common = {
        "win": f(np.asarray(w_in).reshape(8, 128, 56, 128).transpose(2, 1, 0, 3)),
        "wout": f(np.asarray(w_out).reshape(16, 128, 1024)),
        "prew": pk(pre_norm_w),
        "cw": f(np.asarray(conv_w).reshape(3, 24, 128).transpose(2, 1, 0).reshape(128, 72)),
        "cb": pk(conv_b), "skip": pk(hyena_skip), "lnw": pk(sgu_norm_w), "lnb": pk(sgu_norm_b),
        "sgub": f(sgu_b), "sguwT": f(np.asarray(sgu_w).transpose(0, 2, 1)),
        "postw": f(np.asarray(post_norm_w).reshape(1, 1024)),
        "fw1": f(filt_w1), "fw2": f(filt_w2), "fw3": f(filt_w3), "fwo": f(filt_w_out),
        "ffb": f(np.stack([filt_freq1, filt_b1, filt_freq2, filt_b2, filt_freq3, filt_b3], axis=1)),
        "zT": zT, "tl": tl, "ndelta": ndelta,
    }
    in_maps = []
    for i in range(8):
        b, h = i // 2, i % 2
        xp = np.zeros((L + 2, D), np.float32)
        xp[1:L + 1] = x[b]

        def xt(lo):
            blk = xp[lo:lo + 4098]
            return f(blk.T.reshape(8, 128, 4098).transpose(1, 0, 2))
        tb, tf = _tables(h)
        m = dict(common)
        m["xT_s"] = xt(NT * h)
        m["xT_o"] = xt(NT * (1 - h))
        m["xn"] = f(x[b, NT * h:NT * h + NT].reshape(32, 128, 1024))
        m["tabs_b"] = tb
        m["tabs_f"] = tf
        in_maps.append(m)
    if os.environ.get("KRAW", "0") == "1":
        return run_bass_kernel_spmd(nc, in_maps, core_ids=list(range(8)))
    res = run_bass_kernel_spmd(nc, in_maps, core_ids=list(range(8)))
    outp = np.zeros((4, L, D), np.float32)
    for i in range(8):
        b, h = i // 2, i % 2
        outp[b, NT * h:NT * h + NT] = np.asarray(res.results[i]["out"]).reshape(NT, D)
    return outp
```

```python
import math
import os
import contextlib
import numpy as np
import concourse.bass as bass
import concourse.mybir as mybir
from concourse.bass_utils import run_bass_kernel_spmd

F32 = mybir.dt.float32
BF16 = mybir.dt.bfloat16
AF = mybir.ActivationFunctionType
ALU = mybir.AluOpType
AX = mybir.AxisListType

D = 1024
L = 8192
NT = 4096
NF = 16384
EPS = 1e-6
NDQ = 12


class Tok:
    __slots__ = ("key", "val", "eng")

    def __init__(self, key, val, eng):
        self.key, self.val, self.eng = key, val, eng


class Buf:
    def __init__(self):
        self.lw = None
        self.rd = {}


class Plan:
    def __init__(self, nc):
        self.nc = nc
        self.names = ["pe", "act", "dve", "pool", "sp"]
        self.streams = {n: [] for n in self.names}
        self.cnt = {n: 0 for n in self.names}
        self.waited = {n: {} for n in self.names}
        self.stack = contextlib.ExitStack()
        self.semh = {n: self.stack.enter_context(nc.semaphore("s_" + n)) for n in self.names}
        self.dq = {}
        for q in ("sp", "pool"):
            for k in range(NDQ):
                key = "dq_%s_%d" % (q, k)
                self.semh[key] = self.stack.enter_context(nc.semaphore(key))
                self.cnt[key] = 0
            self.dq[q] = 0

    def _deps(self, stream, r, w, skip_same):
        deps = []
        for b in r:
            if b.lw is not None:
                deps.append(b.lw)
        for b in w:
            deps.extend(b.rd.values())
            if b.lw is not None:
                deps.append(b.lw)
        waits = []
        wd = self.waited[stream]
        for t in deps:
            if skip_same and t.eng == stream:
                continue
            if wd.get(t.key, 0) >= t.val:
                continue
            wd[t.key] = t.val
            waits.append((t.key, t.val))
        return waits

    def op(self, stream, fn, r=(), w=()):
        waits = self._deps(stream, r, w, stream == "pe")
        self.cnt[stream] += 1
        tok = Tok(stream, self.cnt[stream], stream)
        self.streams[stream].append((waits, fn, (stream, 1)))
        for b in r:
            b.rd[stream] = tok
        for b in w:
            b.lw = tok
            b.rd = {}
        return tok

    def dma(self, q, out, in_, r=(), w=()):
        waits = self._deps(q, r, w, False)
        k = self.dq[q]
        self.dq[q] = (k + 1) % NDQ
        key = "dq_%s_%d" % (q, k)
        self.cnt[key] += 16
        tok = Tok(key, self.cnt[key], None)
        self.streams[q].append((waits, lambda e, o=out, i=in_: e.dma_start(out=o, in_=i), (key, 16)))
        for b in r:
            b.rd[key] = tok
        for b in w:
            b.lw = tok
            b.rd = {}
        return tok

    def barrier(self):
        toks = [(n, self.cnt[n]) for n in self.names if self.cnt[n] > 0]
        for q in ("sp", "pool"):
            for k in range(NDQ):
                key = "dq_%s_%d" % (q, k)
                if self.cnt[key] > 0:
                    toks.append((key, self.cnt[key]))
        for n in self.names:
            waits = []
            for key, val in toks:
                if self.waited[n].get(key, 0) >= val:
                    continue
                self.waited[n][key] = val
                waits.append((key, val))
            self.streams[n].append((waits, None, None))

    def final_waits(self, stream):
        waits = []
        for q in ("sp", "pool"):
            for k in range(NDQ):
                key = "dq_%s_%d" % (q, k)
                if self.cnt[key] > 0:
                    waits.append((key, self.cnt[key]))
        self.streams[stream].append((waits, None, None))

    def emit(self, stream, e):
        for waits, fn, inc in self.streams[stream]:
            for key, val in waits:
                e.wait_ge(self.semh[key], val)
            if fn is None:
                continue
            ins = fn(e)
            ins.then_inc(self.semh[inc[0]], inc[1])


class Arena:
    def __init__(self, t, dtype, n):
        self.t, self.dtype, self.n = t, dtype, n
        self.base = 0
        self.off = 0

    def mark(self):
        self.base = self.off

    def reset(self):
        self.off = self.base

    def alloc(self, *shape, parts=128):
        n = int(np.prod(shape))
        n4 = (n + 15) // 16 * 16
        assert self.off + n4 <= self.n, ("arena overflow", self.off, n4, self.n)
        ap = self.t[0:parts, self.off:self.off + n]
        self.off += n4
        if len(shape) == 2:
            ap = ap.rearrange("p (a b) -> p a b", a=shape[0])
        elif len(shape) == 3:
            ap = ap.rearrange("p (a b c) -> p a b c", a=shape[0], b=shape[1])
        return ap


def build_program():
    nc = bass.Bass("TRN2", target_bir_lowering=False)

    def din(name, shape, dt=F32):
        return nc.dram_tensor(name, list(shape), dt, kind="ExternalInput")

    xT_o = din("xT_o", [128, 8, 4098])
    xT_s = din("xT_s", [128, 8, 4098])
    xn = din("xn", [32, 128, 1024])
    win = din("win", [56, 128, 8, 128])
    wout = din("wout", [16, 128, 1024])
    prew = din("prew", [128, 8])
    cw = din("cw", [128, 72])
    cb = din("cb", [128, 24])
    skip = din("skip", [128, 8])
    lnw = din("lnw", [128, 8])
    lnb = din("lnb", [128, 8])
    sgub = din("sgub", [8, 128])
    sguwT = din("sguwT", [8, 128, 128])
    postw = din("postw", [1, 1024])
    fw1 = din("fw1", [33, 64])
    fw2 = din("fw2", [64, 64])
    fw3 = din("fw3", [64, 64])
    fwo = din("fwo", [64, 2048])
    ffb = din("ffb", [64, 6])
    zT = din("zT", [33, NF])
    tl = din("tl", [1, NF])
    ndelta = din("ndelta", [128, 8])
    tabs_b = din("tabs_b", [128, 3072], BF16)
    tabs_f = din("tabs_f", [128, 1552])
    out = nc.dram_tensor("out", [32, 128, 1024], F32, kind="ExternalOutput")

    KDEBUG = os.environ.get("KDEBUG", "0") == "1"
    skind = "ExternalOutput" if KDEBUG else "Internal"
    ubuf = nc.dram_tensor("ubuf", [1024, L], BF16, kind=skind)
    wsp = nc.dram_tensor("wsp", [1024, NT], BF16, kind=skind)
    w2sp = nc.dram_tensor("w2sp", [1024, NT], BF16, kind=skind)
    ybsp = nc.dram_tensor("ybsp", [1024, NT], BF16, kind=skind)
    kkd = nc.dram_tensor("kkd", [1024, NF], BF16, kind=skind)
    ycd = nc.dram_tensor("ycd", [1024, NT], BF16, kind=skind)

    NB = 61440
    NFL = 22400
    tb = nc.alloc_sbuf_tensor("arena_b", [128, NB], BF16)
    tf = nc.alloc_sbuf_tensor("arena_f", [128, NFL], F32)
    AB = Arena(tb, BF16, NB)
    AFL = Arena(tf, F32, NFL)
    psum = [nc.alloc_psum_tensor("ps%d" % i, [128, 512], F32) for i in range(8)]
    PB = [Buf() for _ in range(8)]

    P = Plan(nc)
    op, dma = P.op, P.dma
    dbgn = [0]

    def dump(name, ap, buf, dt=F32):
        if not KDEBUG:
            return
        shp = list(ap.shape)
        dtn = nc.dram_tensor("dbg_" + name, shp, dt, kind="ExternalOutput")
        dma("pool", dtn.ap(), ap, r=[buf], w=[Buf()])

    def mm(o, l, r_, st, sp_):
        return lambda e: e.matmul(o, l, r_, start=st, stop=sp_)

    TB = AB.alloc(3072)
    TF = AFL.alloc(1552)
    b_tabs = Buf()
    dma("sp", TB, tabs_b[:, :], w=[b_tabs])
    dma("sp", TF, tabs_f[:, :], w=[b_tabs])
    ones = TB[:, 0:128]
    F1a = TB[:, 128:384]
    Wr = TB[:, 384:512]
    Wi = TB[:, 512:640]
    nWi = TB[:, 640:768]
    G1a = TB[:, 768:1024]
    G1b = TB[:, 1024:1280]
    Vr4 = TB[:, 1280:1312]
    nVi4 = TB[:, 1312:1344]
    F1d = TB[:, 1408:1664]
    F1a65 = TB[:, 1664:1794]
    F1d65 = TB[:, 1794:1924]
    Vr4w = TB[:, 1924:1956]
    nVi4w = TB[:, 1956:1988]
    TTb_P = TB[:, 2048:2560]
    TTb_n = TB[:, 2560:2816]
    TTb_p = TB[:, 2816:3072]
    TT_P = TF[:, 0:512]
    TT_n = TF[:, 512:768]
    TT_p = TF[:, 768:1024]
    TT_P65 = TF[:, 1024:1284]
    TT_n65 = TF[:, 1284:1414]
    TT_p65 = TF[:, 1414:1544]

    vec = AFL.alloc(256)
    b_vec = Buf()
    prew_s = vec[:, 0:8]
    cw_s = vec[:, 8:80]
    cb_s = vec[:, 80:104]
    skip_s = vec[:, 104:112]
    lnw_s = vec[:, 112:120]
    lnb_s = vec[:, 120:128]
    ndel_s = vec[:, 128:136]
    inv_s = vec[:, 136:144]
    eps_s = vec[:, 144:145]
    mpi_s = vec[:, 145:146]
    ffb_s = vec[:, 146:152]
    frb_s = vec[:, 152:155]
    zero_s = vec[:, 155:156]
    for dst, src in ((prew_s, prew), (cw_s, cw), (cb_s, cb), (skip_s, skip), (lnw_s, lnw),
                     (lnb_s, lnb), (ndel_s, ndelta)):
        dma("sp", dst, src[:, :], w=[b_vec])
    dma("sp", ffb_s[0:64, :], ffb[:, :], w=[b_vec])
    op("dve", lambda e: e.memset(eps_s, EPS), w=[b_vec])
    op("dve", lambda e: e.memset(mpi_s, -math.pi), w=[b_vec])
    op("dve", lambda e: e.memset(zero_s, 0.0), w=[b_vec])
    for k in range(3):
        op("dve", lambda e, k=k: e.tensor_tensor(frb_s[0:64, k:k + 1], ffb_s[0:64, 2 * k:2 * k + 1],
                                                 ffb_s[0:64, 2 * k + 1:2 * k + 2], ALU.mult),
           r=[b_vec], w=[b_vec])
    l1acc = AFL.alloc(8, 32)
    b_l1 = Buf()
    op("dve", lambda e: e.memset(l1acc, 0.0), w=[b_l1])
    AB.mark()
    AFL.mark()

    hT = AB.alloc(8, 4098)
    b_hT = Buf()
    xsts = [AFL.alloc(8, 512) for _ in range(2)]
    b_xsts = [Buf(), Buf()]
    sq = AB.alloc(8, 512)
    b_sq = Buf()
    rstd = AFL.alloc(512)
    b_rstd = Buf()
    wst1 = AFL.alloc(8, 512)
    wst = [wst1, wst1]
    b_wst1 = Buf()
    b_wst = [b_wst1, b_wst1]
    wbf = [AB.alloc(4, 8, 128) for _ in range(2)]
    b_wbf = [Buf() for _ in range(2)]
    NSL = 2
    Pf = [AB.alloc(2, 514) for _ in range(NSL)]
    b_Pf = [[Buf(), Buf()] for _ in range(NSL)]
    cacc = [AFL.alloc(2, 512) for _ in range(NSL)]
    b_cacc = [[Buf(), Buf()] for _ in range(NSL)]
    sgs = [AB.alloc(512) for _ in range(NSL)]
    b_sgs = [Buf() for _ in range(NSL)]
    uts = [AB.alloc(512) for _ in range(NSL)]
    b_uts = [Buf() for _ in range(NSL)]
    wts = [AB.alloc(512) for _ in range(NSL)]
    b_wts = [Buf() for _ in range(NSL)]
    w2ts = [AB.alloc(512) for _ in range(NSL)]
    b_w2ts = [Buf() for _ in range(NSL)]
    b_hal = [PB[4], PB[5]]
    b_ubuf, b_wsp, b_w2sp, b_ybsp, b_kkd, b_ycd = Buf(), Buf(), Buf(), Buf(), Buf(), Buf()
    wcount = [0]
    ucount = [0]

    def preprocess(xT_d):
        chunks = [(512 * k, 512) for k in range(8)] + [(4096, 2)]
        for ci, (c0, n) in enumerate(chunks):
            xst, b_xst = xsts[ci % 2], b_xsts[ci % 2]
            dma("sp", xst[:, :, 0:n], xT_d[:, :, c0:c0 + n], w=[b_xst])
            op("act", lambda e, n=n, xst=xst: e.activation(sq[:, :, 0:n], xst[:, :, 0:n], AF.Square),
               r=[b_xst], w=[b_sq])
            for kt in range(8):
                op("pe", mm(psum[7][:, 0:n], ones, sq[:, kt, 0:n], kt == 0, kt == 7),
                   r=[b_sq, b_tabs], w=[PB[7]])
            op("act", lambda e, n=n: e.activation(rstd[:, 0:n], psum[7][:, 0:n], AF.Sqrt,
                                                  bias=eps_s, scale=1.0 / D),
               r=[PB[7], b_vec], w=[b_rstd])
            op("dve", lambda e, n=n: e.reciprocal(rstd[:, 0:n], rstd[:, 0:n]), r=[b_rstd], w=[b_rstd])
            for kt in range(8):
                op("dve", lambda e, n=n, kt=kt, c0=c0, xst=xst: e.tensor_tensor(
                    hT[:, kt, c0:c0 + n], xst[:, kt, 0:n], rstd[:, 0:n], ALU.mult),
                   r=[b_xst, b_rstd], w=[b_hT])

    def load_w(cts):
        ws = wcount[0] % 2
        wcount[0] += 1
        for i, ct in enumerate(cts):
            dma("sp", wst[ws][:, :, 128 * i:128 * i + 128], win[ct, :, :, :], w=[b_wst[ws]])
        n = len(cts)
        for kt in range(8):
            op("act", lambda e, kt=kt, n=n, ws=ws: e.activation(
                wbf[ws][:, 0:n, kt, :], wst[ws][:, kt, 0:128 * n].rearrange("p (a b) -> p a b", a=n),
                AF.Copy, scale=prew_s[:, kt:kt + 1]),
               r=[b_wst[ws], b_vec], w=[b_wbf[ws]])
        return ws

    def proj_fm(bank, ws, wi, c0):
        for kt in range(8):
            op("pe", mm(psum[bank][:, :], wbf[ws][:, wi, kt, :], hT[:, kt, c0:c0 + 512], kt == 0, kt == 7),
               r=[b_wbf[ws], b_hT], w=[PB[bank]])

    def halo_proj(ws, wi, sl, off, j):
        for kt in range(8):
            op("pe", mm(psum[4 + sl][:, off:off + 2], wbf[ws][:, wi, kt, :],
                        hT[:, kt, 512 * j:512 * j + 514:513], kt == 0, kt == 7),
               r=[b_wbf[ws], b_hT], w=[b_hal[sl]])

    def evac_pf(sl, k, bank, off):
        op("act", lambda e: e.activation(Pf[sl][:, k, 1:513], psum[bank][:, :], AF.Copy),
           r=[PB[bank]], w=[b_Pf[sl][k]])
        op("act", lambda e: e.activation(Pf[sl][:, k, 0:514:513], psum[4 + sl][:, off:off + 2],
                                         AF.Copy), r=[b_hal[sl]], w=[b_Pf[sl][k]])

    def unitA(ws, g, j, tbase, sl):
        bA, bB = 2 * sl, 2 * sl + 1
        c0 = 1 + 512 * j
        proj_fm(bA, ws, 0, c0)
        proj_fm(bB, ws, 1, c0)
        halo_proj(ws, 0, sl, 0, j)
        halo_proj(ws, 1, sl, 2, j)
        evac_pf(sl, 0, bA, 0)
        evac_pf(sl, 1, bB, 2)
        cts = (8 + g, 16 + g)
        for tap in range(3):
            for k in range(2):
                ct = cts[k]
                acc = cacc[sl][:, k, :]
                if tap == 0:
                    bank = (bA, bB)[k]
                    op("act", lambda e, acc=acc, bank=bank, ct=ct: e.activation(
                        acc, psum[bank][:, :], AF.Identity, bias=cb_s[:, ct:ct + 1],
                        scale=cw_s[:, 3 * ct + 1:3 * ct + 2]), r=[PB[bank], b_vec], w=[b_cacc[sl][k]])
                else:
                    lo = 0 if tap == 1 else 2
                    wi_ = 3 * ct + (0 if tap == 1 else 2)
                    op("dve", lambda e, acc=acc, k=k, lo=lo, wi_=wi_: e.scalar_tensor_tensor(
                        acc, Pf[sl][:, k, lo:lo + 512], cw_s[:, wi_:wi_ + 1], acc, ALU.mult, ALU.add),
                       r=[b_Pf[sl][k], b_vec], w=[b_cacc[sl][k]])
        op("dve", lambda e: e.tensor_tensor(uts[sl], cacc[sl][:, 0, :], cacc[sl][:, 1, :], ALU.mult),
           r=[b_cacc[sl][0], b_cacc[sl][1]], w=[b_uts[sl]])
        dma("pool", ubuf[128 * g:128 * g + 128, tbase + 512 * j:tbase + 512 * j + 512], uts[sl],
            r=[b_uts[sl]], w=[b_ubuf])

    def unitB(ws, g, j, sl, slA):
        bA, bB = 2 * sl, 2 * sl + 1
        c0 = 1 + 512 * j
        proj_fm(bA, ws, 2, c0)
        proj_fm(bB, ws, 3, c0)
        halo_proj(ws, 2, sl, 0, j)
        evac_pf(sl, 0, bA, 0)
        op("act", lambda e: e.activation(sgs[sl], psum[bB][:, :], AF.Silu), r=[PB[bB]], w=[b_sgs[sl]])
        ct = g
        acc = cacc[sl][:, 0, :]
        pf = Pf[sl]
        op("act", lambda e: e.activation(acc, psum[bA][:, :], AF.Identity, bias=cb_s[:, ct:ct + 1],
                                         scale=cw_s[:, 3 * ct + 1:3 * ct + 2]),
           r=[PB[bA], b_vec], w=[b_cacc[sl][0]])
        for lo, wi_ in ((0, 3 * ct), (2, 3 * ct + 2)):
            op("dve", lambda e, lo=lo, wi_=wi_: e.scalar_tensor_tensor(
                acc, pf[:, 0, lo:lo + 512], cw_s[:, wi_:wi_ + 1], acc, ALU.mult, ALU.add),
               r=[b_Pf[sl][0], b_vec], w=[b_cacc[sl][0]])
        op("dve", lambda e: e.tensor_tensor(wts[sl], acc, sgs[sl], ALU.mult),
           r=[b_cacc[sl][0], b_sgs[sl]], w=[b_wts[sl]])
        op("dve", lambda e: e.scalar_tensor_tensor(w2ts[sl], uts[slA], skip_s[:, g:g + 1], wts[sl],
                                                   ALU.mult, ALU.mult),
           r=[b_uts[slA], b_wts[sl], b_vec], w=[b_w2ts[sl]])
        cs_ = slice(512 * j, 512 * j + 512)
        dma("pool", wsp[128 * g:128 * g + 128, cs_], wts[sl], r=[b_wts[sl]], w=[b_wsp])
        dma("pool", w2sp[128 * g:128 * g + 128, cs_], w2ts[sl], r=[b_w2ts[sl]], w=[b_w2sp])

    def hyena_pass(own):
        tbase = 0 if own else NT

        def cts_of(g):
            return [8 + g, 16 + g, g, 24 + g] if own else [8 + g, 16 + g]
        nxt = load_w(cts_of(0))
        for g in range(8):
            ws = nxt
            if g + 1 < 8:
                nxt = load_w(cts_of(g + 1))
            for j in range(8):
                sl = ucount[0] % NSL
                ucount[0] += 1
                unitA(ws, g, j, tbase, sl)
                if own:
                    sl2 = ucount[0] % NSL
                    ucount[0] += 1
                    unitB(ws, g, j, sl2, sl)

    def two(fn):
        return [fn() for _ in range(2)]
    vsb = two(lambda: AFL.alloc(512))
    b_vsb = two(Buf)
    vsq = two(lambda: AFL.alloc(512))
    b_vsq = two(Buf)
    st = two(lambda: AFL.alloc(16))
    b_st = two(Buf)
    nrm = two(lambda: AB.alloc(512))
    b_nrm = [[Buf() for _ in range(4)] for _ in range(2)]
    mixed = two(lambda: AFL.alloc(512))
    b_mixed = [[Buf() for _ in range(4)] for _ in range(2)]
    tmpf = two(lambda: AFL.alloc(512))
    b_tmpf = two(Buf)
    ybt = two(lambda: AB.alloc(512))
    b_ybt = two(Buf)
    sgq = two(lambda: AB.alloc(512))
    b_sgq = two(Buf)
    wsT = two(lambda: AB.alloc(128))
    b_wsT = two(Buf)
    wsTf = two(lambda: AFL.alloc(128))
    b_wsTf = two(Buf)
    sgb_bc = two(lambda: AFL.alloc(128))
    b_sgb = two(Buf)
    C2 = two(lambda: AFL.alloc(128))
    b_C2 = two(Buf)
    gcount = [0]

    def gmlp_head_prep(hd):
        hs = hd % 2
        dma("sp", wsTf[hs], sguwT[hd, :, :], w=[b_wsTf[hs]])
        op("act", lambda e: e.activation(wsT[hs], wsTf[hs], AF.Copy), r=[b_wsTf[hs]], w=[b_wsT[hs]])
        dma("sp", sgb_bc[hs], sgub[hd:hd + 1, :].partition_broadcast(128).rearrange("p o n -> p (o n)"),
            w=[b_sgb[hs]])

    def gmlp_tile(ws, hd, j, sl):
        hs = hd % 2
        bU, bG, bV, bM = 4 * sl, 4 * sl + 1, 4 * sl + 2, 4 * sl + 3
        c0 = 1 + 512 * j
        S = st[sl]
        bS = b_st[sl]

        def s0():
            for q in range(4):
                for kt in range(8):
                    op("pe", mm(psum[bV][:, 128 * q:128 * q + 128],
                                hT[:, kt, c0 + 128 * q:c0 + 128 * q + 128], wbf[ws][:, 1, kt, :],
                                kt == 0, kt == 7),
                       r=[b_wbf[ws], b_hT], w=[PB[bV]])
            proj_fm(bU, ws, 0, c0)
            proj_fm(bG, ws, 2, c0)

        def s1():
            op("act", lambda e: e.activation(vsb[sl], psum[bV][:, :], AF.Copy), r=[PB[bV]], w=[b_vsb[sl]])
            op("act", lambda e: e.activation(vsq[sl], psum[bV][:, :], AF.Square), r=[PB[bV]], w=[b_vsq[sl]])
            op("act", lambda e: e.activation(sgq[sl], psum[bG][:, :], AF.Silu), r=[PB[bG]], w=[b_sgq[sl]])

        def s2():
            op("dve", lambda e: e.tensor_reduce(S[:, 0:4], vsb[sl].rearrange("p (a b) -> p a b", a=4),
                                                AX.X, ALU.add), r=[b_vsb[sl]], w=[bS])

        def s3():
            op("dve", lambda e: e.tensor_reduce(S[:, 4:8], vsq[sl].rearrange("p (a b) -> p a b", a=4),
                                                AX.X, ALU.add), r=[b_vsq[sl]], w=[bS])

        def s4():
            op("dve", lambda e: e.tensor_scalar(S[:, 8:12], S[:, 0:4], 1.0 / 128, None, ALU.mult),
               r=[bS], w=[bS])

        def s5():
            op("dve", lambda e: e.tensor_tensor(S[:, 12:16], S[:, 8:12], S[:, 8:12], ALU.mult), r=[bS], w=[bS])

        def s6():
            op("dve", lambda e: e.tensor_scalar(S[:, 4:8], S[:, 4:8], 1.0 / 128, None, ALU.mult),
               r=[bS], w=[bS])

        def s7():
            op("dve", lambda e: e.tensor_tensor(S[:, 12:16], S[:, 4:8], S[:, 12:16], ALU.subtract),
               r=[bS], w=[bS])

        def s8():
            op("act", lambda e: e.activation(S[:, 12:16], S[:, 12:16], AF.Sqrt, bias=eps_s, scale=1.0),
               r=[bS, b_vec], w=[bS])

        def s9():
            op("dve", lambda e: e.reciprocal(S[:, 12:16], S[:, 12:16]), r=[bS], w=[bS])

        def s10():
            for q in range(4):
                op("dve", lambda e, q=q: e.tensor_scalar(
                    nrm[sl][:, 128 * q:128 * q + 128], vsb[sl][:, 128 * q:128 * q + 128],
                    S[:, 8 + q:9 + q], S[:, 12 + q:13 + q], ALU.subtract, ALU.mult),
                   r=[b_vsb[sl], bS], w=[b_nrm[sl][q]])

        def s11():
            for q in range(4):
                op("pe", mm(psum[bM][:, 128 * q:128 * q + 128], nrm[sl][:, 128 * q:128 * q + 128], wsT[hs],
                            True, True), r=[b_nrm[sl][q], b_wsT[hs]], w=[PB[bM]])

        def s12():
            for q in range(4):
                op("dve", lambda e, q=q: e.scalar_tensor_tensor(
                    mixed[sl][:, 128 * q:128 * q + 128], psum[bM][:, 128 * q:128 * q + 128],
                    lnw_s[:, hd:hd + 1], C2[hs], ALU.mult, ALU.add),
                   r=[PB[bM], b_C2[hs], b_vec], w=[b_mixed[sl][q]])

        def s13():
            op("dve", lambda e: e.tensor_tensor(tmpf[sl], psum[bU][:, :], mixed[sl], ALU.mult),
               r=[PB[bU]] + b_mixed[sl], w=[b_tmpf[sl]])

        def s14():
            op("dve", lambda e: e.tensor_tensor(ybt[sl], tmpf[sl], sgq[sl], ALU.mult),
               r=[b_tmpf[sl], b_sgq[sl]], w=[b_ybt[sl]])
            dma("pool", ybsp[128 * hd:128 * hd + 128, 512 * j:512 * j + 512], ybt[sl],
                r=[b_ybt[sl]], w=[b_ybsp])

        return [s0, s1, s2, s3, s4, s5, s6, s7, s8, s9, s10, s11, s12, s13, s14]

    def gmlp_pass():
        nxt = load_w([32, 40, 48])
        gmlp_head_prep(0)
        for hd in range(8):
            ws = nxt
            hs = hd % 2
            op("pe", mm(psum[7][:, 0:128], ones, wsT[hs], True, True), r=[b_wsT[hs], b_tabs], w=[PB[7]])
            op("dve", lambda e, hd=hd, hs=hs: e.scalar_tensor_tensor(
                C2[hs], psum[7][:, 0:128], lnb_s[:, hd:hd + 1], sgb_bc[hs], ALU.mult, ALU.add),
               r=[PB[7], b_sgb[hs], b_vec], w=[b_C2[hs]])
            if hd + 1 < 8:
                nxt = load_w([32 + hd + 1, 40 + hd + 1, 48 + hd + 1])
                gmlp_head_prep(hd + 1)
            for jp in range(4):
                sa = gmlp_tile(ws, hd, 2 * jp, 0)
                sb = gmlp_tile(ws, hd, 2 * jp + 1, 1)
                for k in range(len(sa)):
                    sa[k]()
                    sb[k]()

    KSTOP = int(os.environ.get("KSTOP", "9"))

    def finish():
        P.final_waits("pool")
        with nc.Block() as block:
            @block.tensor
            def _(e):
                P.emit("pe", e)

            @block.scalar
            def _(e):
                P.emit("act", e)

            @block.vector
            def _(e):
                P.emit("dve", e)

            @block.gpsimd
            def _(e):
                P.emit("pool", e)

            @block.sync
            def _(e):
                P.emit("sp", e)
        P.stack.close()
        return nc

    preprocess(xT_o)
    if KSTOP == 0:
        return finish()
    hyena_pass(False)
    if KSTOP == 1:
        return finish()
    preprocess(xT_s)
    hyena_pass(True)
    P.barrier()
    gmlp_pass()
    if KSTOP == 2:
        return finish()
    P.barrier()
    AB.reset()
    AFL.reset()

    w1s = AFL.alloc(64)
    w2s = AFL.alloc(64)
    w3s = AFL.alloc(64)
    wos = AFL.alloc(2048)
    b_fw = Buf()
    dma("sp", w1s[0:33, :], fw1[:, :], w=[b_fw])
    dma("sp", w2s[0:64, :], fw2[:, :], w=[b_fw])
    dma("sp", w3s[0:64, :], fw3[:, :], w=[b_fw])
    dma("sp", wos[0:64, :], fwo[:, :], w=[b_fw])
    wos_b = AB.alloc(2048)
    b_wosb = Buf()
    op("act", lambda e: e.activation(wos_b[0:64, :], wos[0:64, :], AF.Copy), r=[b_fw], w=[b_wosb])
    h3b = AB.alloc(2048)
    zb = AFL.alloc(2048)
    b_zb = Buf()
    tlb = AFL.alloc(2048)
    b_tlb = Buf()
    hbuf = [AFL.alloc(2048) for _ in range(3)]
    b_h = [Buf() for _ in range(3)]
    argbs = [AFL.alloc(512) for _ in range(2)]
    b_args = [Buf(), Buf()]
    argks = [AFL.alloc(512) for _ in range(2)]
    b_argks = [Buf(), Buf()]
    decbs = [AFL.alloc(512) for _ in range(2)]
    b_decs = [Buf(), Buf()]
    kkss = [AB.alloc(2048) for _ in range(2)]
    b_kkss = [[Buf() for _ in range(4)] for _ in range(2)]
    junks = [AFL.alloc(512) for _ in range(2)]
    b_junks = [Buf(), Buf()]
    b_l1list = []

    def b_l1x():
        b_l1list.append(Buf())
        return b_l1list[-1]
    MAGIC = 12582912.0
    mlp_banks = [0, 2, 3]
    out_banks = [4, 5, 6, 7]
    mcount = 0
    ocount = 0
    kcount = 0
    for blk in range(8):
        cs = slice(2048 * blk, 2048 * blk + 2048)
        dma("sp", zb[0:33, :], zT[:, cs], w=[b_zb])
        dma("sp", tlb, tl[0:1, cs].partition_broadcast(128).rearrange("p o n -> p (o n)"), w=[b_tlb])
        srcs = [(zb, b_zb, 33, w1s), (hbuf[0], b_h[0], 64, w2s), (hbuf[1], b_h[1], 64, w3s)]
        for li, (src, bsrc, kk_, wl) in enumerate(srcs):
            for c4 in range(4):
                c_ = slice(512 * c4, 512 * c4 + 512)
                bk = mlp_banks[mcount % 3]
                argb, b_arg = argbs[mcount % 2], b_args[mcount % 2]
                argk, b_argk = argks[mcount % 2], b_argks[mcount % 2]
                mcount += 1
                op("pe", mm(psum[bk][0:64, :], wl[0:kk_, :], src[0:kk_, c_], True, True),
                   r=[b_fw, bsrc], w=[PB[bk]])
                op("dve", lambda e, li=li, bk=bk, argb=argb: e.tensor_scalar(
                    argb[0:64, :], psum[bk][0:64, :], ffb_s[0:64, 2 * li:2 * li + 1], frb_s[0:64, li:li + 1],
                    ALU.mult, ALU.add), r=[PB[bk], b_vec], w=[b_arg])
                op("dve", lambda e, argb=argb, argk=argk: e.tensor_scalar(
                    argk[0:64, :], argb[0:64, :], 1.0 / (2 * math.pi), MAGIC, ALU.mult, ALU.add),
                   r=[b_arg], w=[b_argk])
                op("dve", lambda e, argk=argk: e.tensor_scalar(argk[0:64, :], argk[0:64, :], -MAGIC, 2 * math.pi,
                                                              ALU.add, ALU.mult), r=[b_argk], w=[b_argk])
                op("dve", lambda e, argb=argb, argk=argk: e.tensor_tensor(
                    argb[0:64, :], argb[0:64, :], argk[0:64, :], ALU.subtract), r=[b_arg, b_argk], w=[b_arg])
                hdst = h3b if li == 2 else hbuf[li]
                op("act", lambda e, c_=c_, argb=argb, hdst=hdst: e.activation(
                    hdst[0:64, c_], argb[0:64, :], AF.Sin, bias=zero_s[0:64, :], scale=1.0),
                   r=[b_arg, b_vec], w=[b_h[li]])
        for ct in range(8):
            wcol = (0 if blk < 4 else 1024) + 128 * ct
            kks, b_kks = kkss[kcount % 2], b_kkss[kcount % 2]
            kcount += 1
            for c4 in range(4):
                c_ = slice(512 * c4, 512 * c4 + 512)
                bk = out_banks[ocount % 4]
                decb, b_dec = decbs[ocount % 2], b_decs[ocount % 2]
                junk, b_junk = junks[ocount % 2], b_junks[ocount % 2]
                ocount += 1
                op("pe", mm(psum[bk][:, :], wos_b[0:64, wcol:wcol + 128], h3b[0:64, c_], True, True),
                   r=[b_wosb, b_h[2]], w=[PB[bk]])
                op("act", lambda e, ct=ct, c_=c_, decb=decb: e.activation(decb, tlb[:, c_], AF.Exp,
                                                                         scale=ndel_s[:, ct:ct + 1]),
                   r=[b_tlb, b_vec], w=[b_dec])
                op("dve", lambda e, c_=c_, bk=bk, decb=decb, kks=kks: e.tensor_tensor(
                    kks[:, c_], psum[bk][:, :], decb, ALU.mult), r=[PB[bk], b_dec], w=[b_kks[c4]])
                op("dve", lambda e, ct=ct, blk=blk, c4=c4, c_=c_, kks=kks: e.tensor_reduce(
                    l1acc[:, ct, 4 * blk + c4:4 * blk + c4 + 1], kks[:, c_], AX.X, ALU.add,
                    apply_absolute_value=True),
                   r=[b_kks[c4], b_l1], w=[b_l1x()])
            dma("pool", kkd[128 * ct:128 * ct + 128, cs], kks, r=b_kks, w=[b_kkd])
    op("dve", lambda e: e.tensor_reduce(inv_s, l1acc, AX.X, ALU.add), r=[b_l1] + b_l1list, w=[b_vec])
    op("dve", lambda e: e.tensor_scalar(inv_s, inv_s, EPS, float(NF), ALU.add, ALU.mult), r=[b_vec], w=[b_vec])
    op("dve", lambda e: e.reciprocal(inv_s, inv_s), r=[b_vec], w=[b_vec])
    dump("inv", inv_s, b_vec)
    dump("l1acc", l1acc, b_l1)
    if KSTOP == 3:
        return finish()
    P.barrier()
    AB.reset()
    AFL.reset()

    wo_b = AB.alloc(16, 1024)
    b_wo = Buf()
    wo_st = AFL.alloc(1024)
    b_wost = Buf()
    for ct in range(16):
        dma("sp", wo_st, wout[ct, :, :], w=[b_wost])
        op("act", lambda e, ct=ct: e.activation(wo_b[:, ct, :], wo_st, AF.Copy), r=[b_wost], w=[b_wo])
    AB.mark()
    AFL.mark()
    KKs = [AB.alloc(16, 128) for _ in range(2)]
    b_KKs = [Buf(), Buf()]
    Zs = [AB.alloc(16, 128) for _ in range(2)]
    b_Zs = [Buf(), Buf()]
    Y16s = [AB.alloc(16, 128) for _ in range(2)]
    b_Y16s = [Buf(), Buf()]
    PQs = [AB.alloc(2, 2, 130) for _ in range(4)]
    b_PQs = [[Buf(), Buf(), Buf()] for _ in range(4)]
    PQ2s = [AB.alloc(4, 2, 65) for _ in range(4)]
    b_PQ2s = [[Buf(), Buf(), Buf()] for _ in range(4)]
    PQ3s = [AB.alloc(2, 2, 256) for _ in range(4)]
    b_PQ3s = [[Buf(), Buf(), Buf()] for _ in range(4)]
    Kcs = [AB.alloc(528) for _ in range(4)]
    b_Kcs = [[Buf(), Buf(), Buf()] for _ in range(4)]
    Cbs = [AB.alloc(512) for _ in range(4)]
    b_Cbs = [Buf() for _ in range(4)]
    kkv = kkd.ap().rearrange("c (a p) -> a c p", p=128)
    ubv = ubuf.ap().rearrange("c (a p) -> a c p", p=128)
    ycv = ycd.ap().rearrange("c (a p) -> a c p", p=128)

    def twiddle_f(sl, dst, bdst):
        a = psum[2 * sl][:, 0:260]
        pb = PB[2 * sl]
        a3 = a.rearrange("p (c r k) -> p c r k", c=2, r=2)
        d0 = dst[:, 0, :, :].rearrange("p c k -> p (c k)")
        d1 = dst[:, 1, :, :].rearrange("p c (r k) -> p c r k", r=2)
        op("dve", lambda e: e.tensor_tensor(d0, a, TT_P65, ALU.mult), r=[pb, b_tabs], w=[bdst[0]])
        op("dve", lambda e: e.tensor_tensor(d1[:, :, 0, :], a3[:, :, 1, :],
                                            TT_n65.rearrange("p (c k) -> p c k", c=2), ALU.mult),
           r=[pb, b_tabs], w=[bdst[1]])
        op("dve", lambda e: e.tensor_tensor(d1[:, :, 1, :], a3[:, :, 0, :],
                                            TT_p65.rearrange("p (c k) -> p c k", c=2), ALU.mult),
           r=[pb, b_tabs], w=[bdst[2]])

    def twiddle_i(sl, dst, bdst):
        a = Cbs[sl][0:65, :]
        pb = b_Cbs[sl]
        op("act", lambda e: e.activation(a, psum[2 * sl + 1][0:65, :], AF.Copy), r=[PB[2 * sl + 1]], w=[pb])
        a3 = a.rearrange("p (c r k) -> p c r k", c=2, r=2)
        d0 = dst[0:65, 0, :, :].rearrange("p c k -> p (c k)")
        d1 = dst[0:65, 1, :, :].rearrange("p c (r k) -> p c r k", r=2)
        op("dve", lambda e: e.tensor_tensor(d0, a, TTb_P[0:65, :], ALU.mult), r=[pb, b_tabs], w=[bdst[0]])
        op("dve", lambda e: e.tensor_tensor(d1[:, :, 0, :], a3[:, :, 1, :],
                                            TTb_p[0:65, :].rearrange("p (c k) -> p c k", c=2), ALU.mult),
           r=[pb, b_tabs], w=[bdst[1]])
        op("dve", lambda e: e.tensor_tensor(d1[:, :, 1, :], a3[:, :, 0, :],
                                            TTb_n[0:65, :].rearrange("p (c k) -> p c k", c=2), ALU.mult),
           r=[pb, b_tabs], w=[bdst[2]])

    def stage2b(sl):
        src, bsrc = PQs[sl], b_PQs[sl]
        X = psum[2 * sl + 1]

        def part(pq, r):
            return src[:, pq, :, 65 * r:65 * r + 65]
        seq_r = [(nWi, part(0, 1)), (nWi, part(1, 1)), (Wr, part(0, 0)), (Wr, part(1, 0))]
        seq_i = [(Wr, part(0, 1)), (Wr, part(1, 1)), (Wi, part(0, 0)), (Wi, part(1, 0))]
        for half, seq in ((0, seq_r), (1, seq_i)):
            o = X[:, 130 * half:130 * half + 130].rearrange("p (c k) -> p c k", c=2)
            for i, (wm, rr) in enumerate(seq):
                op("pe", mm(o, wm, rr, i == 0, i == 3), r=bsrc + [b_tabs], w=[PB[2 * sl + 1]])

    def chunk_stages(gs, sl, cc0):
        KK, Z, Y16 = KKs[gs], Zs[gs], Y16s[gs]
        A = psum[2 * sl]
        X = psum[2 * sl + 1]
        C = X
        Yp = A
        pA, pX = PB[2 * sl], PB[2 * sl + 1]
        pC, pY = pX, pA
        Kc, bKc = Kcs[sl], b_Kcs[sl]
        PQ2, bPQ2 = PQ2s[sl], b_PQ2s[sl]
        PQ3, bPQ3 = PQ3s[sl], b_PQ3s[sl]

        def s1f():
            for c in range(2):
                op("pe", mm(A[:, 130 * c:130 * c + 130], KK[:, cc0 + c, :], F1a65, True, True),
                   r=[b_KKs[gs], b_tabs], w=[pA])

        def twf():
            twiddle_f(sl, PQs[sl], b_PQs[sl])

        def s2():
            stage2b(sl)

        def evk():
            op("act", lambda e: e.activation(Kc[:, 0:130], X[:, 0:130], AF.Copy), r=[pX], w=[bKc[0]])
            op("act", lambda e: e.activation(Kc[:, 130:390], X[:, 0:260], AF.Copy), r=[pX], w=[bKc[1]])
            op("act", lambda e: e.activation(Kc[:, 390:520], X[:, 130:260], AF.Copy, scale=-1.0),
               r=[pX], w=[bKc[2]])

        def s1d():
            for c in range(2):
                op("pe", mm(A[:, 130 * c:130 * c + 130], Z[0:64, cc0 + c, :], F1d65[0:64, :], True, True),
                   r=[b_Zs[gs], b_tabs], w=[pA])

        def kmul():
            fl = PQ2.rearrange("p a c k -> p (a c k)")
            op("dve", lambda e: e.tensor_tensor(fl[:, 0:260], X[:, 0:260], Kc[:, 0:260], ALU.mult),
               r=[pX, bKc[0], bKc[1]], w=[bPQ2[0]])
            op("dve", lambda e: e.tensor_tensor(fl[:, 260:390], X[:, 130:260], Kc[:, 390:520], ALU.mult),
               r=[pX, bKc[2]], w=[bPQ2[1]])
            op("dve", lambda e: e.tensor_tensor(fl[:, 390:520], X[:, 0:130], Kc[:, 260:390], ALU.mult),
               r=[pX, bKc[1]], w=[bPQ2[2]])

        def s3():
            for c in range(2):
                o = C[0:65, 256 * c:256 * c + 256]
                seq = [(0, G1a), (2, G1a), (1, G1b), (3, G1b)]
                for i, (pi_, tab) in enumerate(seq):
                    op("pe", mm(o, PQ2[:, pi_, c, :], tab, i == 0, i == 3), r=bPQ2 + [b_tabs], w=[pC])

        def itw():
            twiddle_i(sl, PQ3, bPQ3)

        def s4():
            def part3(pq, r):
                return PQ3[0:65, pq, :, 128 * r:128 * r + 128]
            seq = [(Vr4w[0:65, :], part3(0, 0)), (Vr4w[0:65, :], part3(1, 0)),
                   (nVi4w[0:65, :], part3(0, 1)), (nVi4w[0:65, :], part3(1, 1))]
            for i, (wm, rr) in enumerate(seq):
                op("pe", mm(Yp[0:32, 0:256].rearrange("p (c k) -> p c k", c=2), wm, rr, i == 0, i == 3),
                   r=bPQ3 + [b_tabs], w=[pY])

        def evy():
            op("act", lambda e: e.activation(Y16[0:32, cc0:cc0 + 2, :].rearrange("p c k -> p (c k)"),
                                             Yp[0:32, 0:256], AF.Copy), r=[pY], w=[b_Y16s[gs]])

        def s2_s1d():
            s2()
            s1d()

        return [s1f, twf, s2_s1d, evk, twf, s2, kmul, s3, itw, s4, evy]

    for grp in range(64):
        gs = grp % 2
        ch0 = 16 * grp
        dma("sp", KKs[gs], kkv[:, ch0:ch0 + 16, :], r=[b_kkd], w=[b_KKs[gs]])
        dma("sp", Zs[gs][0:64, :, :], ubv[:, ch0:ch0 + 16, :], r=[b_ubuf], w=[b_Zs[gs]])
        for quad in range(2):
            sts = [chunk_stages(gs, sl, 8 * quad + 2 * sl) for sl in range(4)]
            for k in range(len(sts[0])):
                for sl in range(4):
                    sts[sl][k]()
        dma("pool", ycv[:, ch0:ch0 + 16, :], Y16s[gs][0:32, :, :], r=[b_Y16s[gs]], w=[b_ycd])
    if KSTOP == 4:
        return finish()
    P.barrier()
    AB.reset()
    AFL.reset()

    pw_bc = AFL.alloc(1024)
    b_pw = Buf()
    dma("sp", pw_bc, postw[0:1, :].partition_broadcast(128).rearrange("p o n -> p (o n)"), w=[b_pw])
    yc_t = two(lambda: AB.alloc(8, 512))
    b_yct = two(Buf)
    w_t = two(lambda: AB.alloc(8, 512))
    b_w_t = two(Buf)
    w2_t = two(lambda: AB.alloc(8, 512))
    b_w2_t = two(Buf)
    ya_t = two(lambda: AB.alloc(8, 512))
    b_ya = [[Buf() for _ in range(8)] for _ in range(2)]
    yb_t = two(lambda: AB.alloc(8, 512))
    b_yb = two(Buf)
    N3 = 3
    xt_ = [AFL.alloc(1024) for _ in range(N3)]
    b_xt = [Buf() for _ in range(N3)]
    ot = [AFL.alloc(1024) for _ in range(N3)]
    b_ot = [[Buf(), Buf()] for _ in range(N3)]
    sq2 = [AFL.alloc(1024) for _ in range(N3)]
    b_sq2 = [Buf() for _ in range(N3)]
    ss = [AFL.alloc(4) for _ in range(N3)]
    b_ss = [Buf() for _ in range(N3)]

    def v3(d):
        return d.ap().rearrange("(g c) t -> c g t", c=128)

    scount = 0
    for j in range(8):
        jp = j % 2
        ts_ = slice(512 * j, 512 * j + 512)
        dma("sp", yc_t[jp], v3(ycd)[:, :, ts_], r=[b_ycd], w=[b_yct[jp]])
        dma("sp", w_t[jp], v3(wsp)[:, :, ts_], r=[b_wsp], w=[b_w_t[jp]])
        dma("sp", w2_t[jp], v3(w2sp)[:, :, ts_], r=[b_w2sp], w=[b_w2_t[jp]])
        dma("sp", yb_t[jp], v3(ybsp)[:, :, ts_], r=[b_ybsp], w=[b_yb[jp]])
        for g in range(8):
            op("dve", lambda e, g=g, jp=jp: e.scalar_tensor_tensor(
                ya_t[jp][:, g, :], yc_t[jp][:, g, :], inv_s[:, g:g + 1], w_t[jp][:, g, :], ALU.mult, ALU.mult),
               r=[b_yct[jp], b_w_t[jp], b_vec], w=[b_ya[jp][g]])
        for g in range(8):
            op("dve", lambda e, g=g, jp=jp: e.tensor_tensor(ya_t[jp][:, g, :], ya_t[jp][:, g, :],
                                                           w2_t[jp][:, g, :], ALU.add),
               r=[b_w2_t[jp]], w=[b_ya[jp][g]])
        for q in range(4):
            tix = 4 * j + q
            k3 = scount % N3
            scount += 1
            b0 = 2 * k3
            dma("sp", xt_[k3], xn[tix, :, :], w=[b_xt[k3]])
            for hf in range(2):
                pbk = PB[b0 + hf]
                for ct in range(16):
                    if ct < 8:
                        src, bsrc = ya_t[jp][:, ct, 128 * q:128 * q + 128], b_ya[jp][ct]
                    else:
                        src, bsrc = yb_t[jp][:, ct - 8, 128 * q:128 * q + 128], b_yb[jp]
                    op("pe", mm(psum[b0 + hf][:, :], src, wo_b[:, ct, 512 * hf:512 * hf + 512], ct == 0, ct == 15),
                       r=[bsrc, b_wo], w=[pbk])
            op("dve", lambda e, k3=k3: e.memset(ss[k3], 0.0), w=[b_ss[k3]])
            for hf in range(2):
                op("act", lambda e, hf=hf, k3=k3, b0=b0: e.activation(
                    sq2[k3][:, 512 * hf:512 * hf + 512], psum[b0 + hf][:, :],
                    AF.Square, accum_out=ss[k3][:, hf:hf + 1]),
                   r=[PB[b0 + hf]], w=[b_sq2[k3], b_ss[k3]])
            op("dve", lambda e, k3=k3: e.tensor_tensor(ss[k3][:, 2:3], ss[k3][:, 0:1], ss[k3][:, 1:2], ALU.add),
               r=[b_ss[k3]], w=[b_ss[k3]])
            op("act", lambda e, k3=k3: e.activation(ss[k3][:, 3:4], ss[k3][:, 2:3], AF.Sqrt, bias=eps_s,
                                                    scale=1.0 / D), r=[b_ss[k3], b_vec], w=[b_ss[k3]])
            op("dve", lambda e, k3=k3: e.reciprocal(ss[k3][:, 3:4], ss[k3][:, 3:4]), r=[b_ss[k3]], w=[b_ss[k3]])
            for hf in range(2):
                op("dve", lambda e, hf=hf, k3=k3, b0=b0: e.scalar_tensor_tensor(
                    ot[k3][:, 512 * hf:512 * hf + 512], psum[b0 + hf][:, :], ss[k3][:, 3:4],
                    pw_bc[:, 512 * hf:512 * hf + 512], ALU.mult, ALU.mult),
                   r=[PB[b0 + hf], b_ss[k3], b_pw], w=[b_ot[k3][hf]])
            for hf in range(2):
                op("dve", lambda e, hf=hf, k3=k3: e.tensor_tensor(
                    ot[k3][:, 512 * hf:512 * hf + 512], ot[k3][:, 512 * hf:512 * hf + 512],
                    xt_[k3][:, 512 * hf:512 * hf + 512], ALU.add),
                   r=[b_xt[k3]], w=[b_ot[k3][hf]])
            dma("pool", out[tix, :, :], ot[k3], r=b_ot[k3], w=[Buf()])
    P.final_waits("pool")

    with nc.Block() as block:
        @block.tensor
        def _(e):
            P.emit("pe", e)

        @block.scalar
        def _(e):
            P.emit("act", e)

        @block.vector
        def _(e):
            P.emit("dve", e)

        @block.gpsimd
        def _(e):
            P.emit("pool", e)

        @block.sync
        def _(e):
            P.emit("sp", e)
    P.stack.close()
    return nc


def _tables(h):
    n = np.arange(128)
    ang = 2 * np.pi * np.outer(n, n) / 128.0
    wr, wi = np.cos(ang), -np.sin(ang)
    tb = np.zeros((128, 3072), np.float64)
    tb[:, 0:128] = 1.0
    tb[:, 128:256], tb[:, 256:384] = wr, wi
    tb[:, 384:512], tb[:, 512:640], tb[:, 640:768] = wr, wi, -wi
    tb[:, 768:896], tb[:, 896:1024] = wr, -wi
    tb[:, 1024:1152], tb[:, 1152:1280] = wi, wr
    own = np.arange(32 * h, 32 * h + 32)
    tb[:, 1280:1312] = wr[:, own]
    tb[:, 1312:1344] = wi[:, own]
    perm = np.array([(32 * h + a) if a < 32 else (32 * (1 - h) + a - 32) for a in range(64)])
    tb[0:64, 1408:1536] = wr[perm, :]
    tb[0:64, 1536:1664] = wi[perm, :]
    tb[:, 1664:1729], tb[:, 1729:1794] = wr[:, 0:65], wi[:, 0:65]
    tb[0:64, 1794:1859], tb[0:64, 1859:1924] = wr[perm, 0:65], wi[perm, 0:65]
    wgt = np.full((65, 1), 2.0)
    wgt[0, 0] = 1.0
    wgt[64, 0] = 1.0
    tb[0:65, 1924:1956] = wgt * wr[0:65][:, own]
    tb[0:65, 1956:1988] = wgt * wi[0:65][:, own]
    ang2 = 2 * np.pi * np.outer(n, n) / float(NF)
    tr, ti = np.cos(ang2), -np.sin(ang2)
    t65, i65 = tr[:, 0:65], ti[:, 0:65]
    tb[:, 2048:3072] = np.concatenate([tr, tr, tr, tr, -ti, -ti, ti, ti], axis=1)
    tf = np.concatenate([tr, tr, tr, tr, -ti, -ti, ti, ti, t65, t65, t65, t65, -i65, -i65, i65, i65,
                         np.zeros((128, 8))], axis=1)
    import ml_dtypes
    return tb.astype(np.float32).astype(ml_dtypes.bfloat16), tf.astype(np.float32)


def _consts():
    bands = 16
    t = np.linspace(0.0, 1.0, L, dtype=np.float32)[:, None]
    w = (2.0 * np.float32(math.pi) * np.arange(L, dtype=np.float32)[:, None] / np.float32(L)).astype(np.float32)
    f = np.linspace(1e-4, bands - 1, bands, dtype=np.float32)[None, :]
    z = np.concatenate([t, np.cos(f * w), -np.sin(f * w)], axis=-1).astype(np.float32)
    order = np.concatenate([np.arange(L), [0], np.arange(L - 1, 0, -1)])
    zT = np.ascontiguousarray(z[order].T)
    tl = t[order, 0].copy()
    tl[L] = 1.0e4
    deltas = np.abs(np.linspace(math.log(1e-2) / 1.5, math.log(1e-2) / 0.3, D, dtype=np.float32))
    ndelta = np.ascontiguousarray((-deltas).reshape(8, 128).T)
    return zT.astype(np.float32), tl.reshape(1, NF).astype(np.float32), ndelta.astype(np.float32)


def kernel(x, pre_norm_w, w_in, conv_w, conv_b, filt_w1, filt_b1, filt_freq1, filt_w2, filt_b2,
           filt_freq2, filt_w3, filt_b3, filt_freq3, filt_w_out, hyena_skip, sgu_norm_w, sgu_norm_b,
           sgu_w, sgu_b, w_out, post_norm_w):
    f = lambda a: np.ascontiguousarray(np.asarray(a, dtype=np.float32))
    x = f(x)
    nc = build_program()
    zT, tl, ndelta = _consts()
    pk = lambda v: f(np.asarray(v).reshape(-1, 128).T)
    common = {
        "win": f(np.asarray(w_in).reshape(8, 128, 56, 128).transpose(2, 1, 0, 3)),
        "wout": f(np.asarray(w_out).reshape(16, 128, 1024)),
        "prew": pk(pre_norm_w),
        "cw": f(np.asarray(conv_w).reshape(3, 24, 128).transpose(2, 1, 0).reshape(128, 72)),
        "cb": pk(conv_b), "skip": pk(hyena_skip), "lnw": pk(sgu_norm_w), "lnb": pk(sgu_norm_b),
        "sgub": f(sgu_b), "sguwT": f(np.asarray(sgu_w).transpose(0, 2, 1)),
        "postw": f(np.asarray(post_norm_w).reshape(1, 1024)),
        "fw1": f(filt_w1), "fw2": f(filt_w2), "fw3": f(filt_w3), "fwo": f(filt_w_out),
        "ffb": f(np.stack([filt_freq1, filt_b1, filt_freq2, filt_b2, filt_freq3, filt_b3], axis=1)),
        "zT": zT, "tl": tl, "ndelta": ndelta,
    }
    in_maps = []
    for i in range(8):
        b, h = i // 2, i % 2
        xp = np.zeros((L + 2, D), np.float32)
        xp[1:L + 1] = x[b]

        def xt(lo):
            blk = xp[lo:lo + 4098]
            return f(blk.T.reshape(8, 128, 4098).transpose(1, 0, 2))
        tb, tf = _tables(h)
        m = dict(common)
        m["xT_s"] = xt(NT * h)
        m["xT_o"] = xt(NT * (1 - h))
        m["xn"] = f(x[b, NT * h:NT * h + NT].reshape(32, 128, 1024))
        m["tabs_b"] = tb
        m["tabs_f"] = tf
        in_maps.append(m)
    if os.environ.get("KRAW", "0") == "1":
        return run_bass_kernel_spmd(nc, in_maps, core_ids=list(range(8)))
    res = run_bass_kernel_spmd(nc, in_maps, core_ids=list(range(8)))
    outp = np.zeros((4, L, D), np.float32)
    for i in range(8):
        b, h = i // 2, i % 2
        outp[b, NT * h:NT * h + NT] = np.asarray(res.results[i]["out"]).reshape(NT, D)
    return outp
```

```python
import math
import os
import contextlib
import numpy as np
import concourse.bass as bass
import concourse.mybir as mybir
from concourse.bass_utils import run_bass_kernel_spmd

F32 = mybir.dt.float32
BF16 = mybir.dt.bfloat16
AF = mybir.ActivationFunctionType
ALU = mybir.AluOpType
AX = mybir.AxisListType

D = 1024
L = 8192
NT = 4096
NF = 16384
EPS = 1e-6
NDQ = 12


class Tok:
    __slots__ = ("key", "val", "eng")

    def __init__(self, key, val, eng):
        self.key, self.val, self.eng = key, val, eng


class Buf:
    def __init__(self):
        self.lw = None
        self.rd = {}


class Plan:
    def __init__(self, nc):
        self.nc = nc
        self.names = ["pe", "act", "dve", "pool", "sp"]
        self.streams = {n: [] for n in self.names}
        self.cnt = {n: 0 for n in self.names}
        self.waited = {n: {} for n in self.names}
        self.stack = contextlib.ExitStack()
        self.semh = {n: self.stack.enter_context(nc.semaphore("s_" + n)) for n in self.names}
        self.dq = {}
        for q in ("sp", "pool"):
            for k in range(NDQ):
                key = "dq_%s_%d" % (q, k)
                self.semh[key] = self.stack.enter_context(nc.semaphore(key))
                self.cnt[key] = 0
            self.dq[q] = 0

    def _deps(self, stream, r, w, skip_same):
        deps = []
        for b in r:
            if b.lw is not None:
                deps.append(b.lw)
        for b in w:
            deps.extend(b.rd.values())
            if b.lw is not None:
                deps.append(b.lw)
        waits = []
        wd = self.waited[stream]
        for t in deps:
            if skip_same and t.eng == stream:
                continue
            if wd.get(t.key, 0) >= t.val:
                continue
            wd[t.key] = t.val
            waits.append((t.key, t.val))
        return waits

    def op(self, stream, fn, r=(), w=()):
        waits = self._deps(stream, r, w, stream == "pe")
        self.cnt[stream] += 1
        tok = Tok(stream, self.cnt[stream], stream)
        self.streams[stream].append((waits, fn, (stream, 1)))
        for b in r:
            b.rd[stream] = tok
        for b in w:
            b.lw = tok
            b.rd = {}
        return tok

    def dma(self, q, out, in_, r=(), w=()):
        waits = self._deps(q, r, w, False)
        k = self.dq[q]
        self.dq[q] = (k + 1) % NDQ
        key = "dq_%s_%d" % (q, k)
        self.cnt[key] += 16
        tok = Tok(key, self.cnt[key], None)
        self.streams[q].append((waits, lambda e, o=out, i=in_: e.dma_start(out=o, in_=i), (key, 16)))
        for b in r:
            b.rd[key] = tok
        for b in w:
            b.lw = tok
            b.rd = {}
        return tok

    def barrier(self):
        toks = [(n, self.cnt[n]) for n in self.names if self.cnt[n] > 0]
        for q in ("sp", "pool"):
            for k in range(NDQ):
                key = "dq_%s_%d" % (q, k)
                if self.cnt[key] > 0:
                    toks.append((key, self.cnt[key]))
        for n in self.names:
            waits = []
            for key, val in toks:
                if self.waited[n].get(key, 0) >= val:
                    continue
                self.waited[n][key] = val
                waits.append((key, val))
            self.streams[n].append((waits, None, None))

    def final_waits(self, stream):
        waits = []
        for q in ("sp", "pool"):
            for k in range(NDQ):
                key = "dq_%s_%d" % (q, k)
                if self.cnt[key] > 0:
                    waits.append((key, self.cnt[key]))
        self.streams[stream].append((waits, None, None))

    def emit(self, stream, e):
        for waits, fn, inc in self.streams[stream]:
            for key, val in waits:
                e.wait_ge(self.semh[key], val)
            if fn is None:
                continue
            ins = fn(e)
            ins.then_inc(self.semh[inc[0]], inc[1])


class Arena:
    def __init__(self, t, dtype, n):
        self.t, self.dtype, self.n = t, dtype, n
        self.base = 0
        self.off = 0

    def mark(self):
        self.base = self.off

    def reset(self):
        self.off = self.base

    def alloc(self, *shape, parts=128):
        n = int(np.prod(shape))
        n4 = (n + 15) // 16 * 16
        assert self.off + n4 <= self.n, ("arena overflow", self.off, n4, self.n)
        ap = self.t[0:parts, self.off:self.off + n]
        self.off += n4
        if len(shape) == 2:
            ap = ap.rearrange("p (a b) -> p a b", a=shape[0])
        elif len(shape) == 3:
            ap = ap.rearrange("p (a b c) -> p a b c", a=shape[0], b=shape[1])
        return ap


def build_program():
    nc = bass.Bass("TRN2", target_bir_lowering=False)

    def din(name, shape, dt=F32):
        return nc.dram_tensor(name, list(shape), dt, kind="ExternalInput")

    xT_o = din("xT_o", [128, 8, 4098])
    xT_s = din("xT_s", [128, 8, 4098])
    xn = din("xn", [32, 128, 1024])
    win = din("win", [56, 128, 8, 128])
    wout = din("wout", [16, 128, 1024])
    prew = din("prew", [128, 8])
    cw = din("cw", [128, 72])
    cb = din("cb", [128, 24])
    skip = din("skip", [128, 8])
    lnw = din("lnw", [128, 8])
    lnb = din("lnb", [128, 8])
    sgub = din("sgub", [8, 128])
    sguwT = din("sguwT", [8, 128, 128])
    postw = din("postw", [1, 1024])
    fw1 = din("fw1", [33, 64])
    fw2 = din("fw2", [64, 64])
    fw3 = din("fw3", [64, 64])
    fwo = din("fwo", [64, 2048])
    ffb = din("ffb", [64, 6])
    zT = din("zT", [33, NF])
    tl = din("tl", [1, NF])
    ndelta = din("ndelta", [128, 8])
    tabs_b = din("tabs_b", [128, 3072], BF16)
    tabs_f = din("tabs_f", [128, 1552])
    out = nc.dram_tensor("out", [32, 128, 1024], F32, kind="ExternalOutput")

    KDEBUG = os.environ.get("KDEBUG", "0") == "1"
    skind = "ExternalOutput" if KDEBUG else "Internal"
    ubuf = nc.dram_tensor("ubuf", [1024, L], BF16, kind=skind)
    wsp = nc.dram_tensor("wsp", [1024, NT], BF16, kind=skind)
    w2sp = nc.dram_tensor("w2sp", [1024, NT], BF16, kind=skind)
    ybsp = nc.dram_tensor("ybsp", [1024, NT], BF16, kind=skind)
    kkd = nc.dram_tensor("kkd", [1024, NF], BF16, kind=skind)
    ycd = nc.dram_tensor("ycd", [1024, NT], BF16, kind=skind)

    NB = 61440
    NFL = 22400
    tb = nc.alloc_sbuf_tensor("arena_b", [128, NB], BF16)
    tf = nc.alloc_sbuf_tensor("arena_f", [128, NFL], F32)
    AB = Arena(tb, BF16, NB)
    AFL = Arena(tf, F32, NFL)
    psum = [nc.alloc_psum_tensor("ps%d" % i, [128, 512], F32) for i in range(8)]
    PB = [Buf() for _ in range(8)]

    P = Plan(nc)
    op, dma = P.op, P.dma
    dbgn = [0]

    def dump(name, ap, buf, dt=F32):
        if not KDEBUG:
            return
        shp = list(ap.shape)
        dtn = nc.dram_tensor("dbg_" + name, shp, dt, kind="ExternalOutput")
        dma("pool", dtn.ap(), ap, r=[buf], w=[Buf()])

    def mm(o, l, r_, st, sp_):
        return lambda e: e.matmul(o, l, r_, start=st, stop=sp_)

    TB = AB.alloc(3072)
    TF = AFL.alloc(1552)
    b_tabs = Buf()
    dma("sp", TB, tabs_b[:, :], w=[b_tabs])
    dma("sp", TF, tabs_f[:, :], w=[b_tabs])
    ones = TB[:, 0:128]
    F1a = TB[:, 128:384]
    Wr = TB[:, 384:512]
    Wi = TB[:, 512:640]
    nWi = TB[:, 640:768]
    G1a = TB[:, 768:1024]
    G1b = TB[:, 1024:1280]
    Vr4 = TB[:, 1280:1312]
    nVi4 = TB[:, 1312:1344]
    F1d = TB[:, 1408:1664]
    F1a65 = TB[:, 1664:1794]
    F1d65 = TB[:, 1794:1924]
    Vr4w = TB[:, 1924:1956]
    nVi4w = TB[:, 1956:1988]
    TTb_P = TB[:, 2048:2560]
    TTb_n = TB[:, 2560:2816]
    TTb_p = TB[:, 2816:3072]
    TT_P = TF[:, 0:512]
    TT_n = TF[:, 512:768]
    TT_p = TF[:, 768:1024]
    TT_P65 = TF[:, 1024:1284]
    TT_n65 = TF[:, 1284:1414]
    TT_p65 = TF[:, 1414:1544]

    vec = AFL.alloc(256)
    b_vec = Buf()
    prew_s = vec[:, 0:8]
    cw_s = vec[:, 8:80]
    cb_s = vec[:, 80:104]
    skip_s = vec[:, 104:112]
    lnw_s = vec[:, 112:120]
    lnb_s = vec[:, 120:128]
    ndel_s = vec[:, 128:136]
    inv_s = vec[:, 136:144]
    eps_s = vec[:, 144:145]
    mpi_s = vec[:, 145:146]
    ffb_s = vec[:, 146:152]
    frb_s = vec[:, 152:155]
    zero_s = vec[:, 155:156]
    for dst, src in ((prew_s, prew), (cw_s, cw), (cb_s, cb), (skip_s, skip), (lnw_s, lnw),
                     (lnb_s, lnb), (ndel_s, ndelta)):
        dma("sp", dst, src[:, :], w=[b_vec])
    dma("sp", ffb_s[0:64, :], ffb[:, :], w=[b_vec])
    op("dve", lambda e: e.memset(eps_s, EPS), w=[b_vec])
    op("dve", lambda e: e.memset(mpi_s, -math.pi), w=[b_vec])
    op("dve", lambda e: e.memset(zero_s, 0.0), w=[b_vec])
    for k in range(3):
        op("dve", lambda e, k=k: e.tensor_tensor(frb_s[0:64, k:k + 1], ffb_s[0:64, 2 * k:2 * k + 1],
                                                 ffb_s[0:64, 2 * k + 1:2 * k + 2], ALU.mult),
           r=[b_vec], w=[b_vec])
    l1acc = AFL.alloc(8, 32)
    b_l1 = Buf()
    op("dve", lambda e: e.memset(l1acc, 0.0), w=[b_l1])
    AB.mark()
    AFL.mark()

    hT = AB.alloc(8, 4098)
    b_hT = Buf()
    xsts = [AFL.alloc(8, 512) for _ in range(2)]
    b_xsts = [Buf(), Buf()]
    sq = AB.alloc(8, 512)
    b_sq = Buf()
    rstd = AFL.alloc(512)
    b_rstd = Buf()
    wst1 = AFL.alloc(8, 512)
    wst = [wst1, wst1]
    b_wst1 = Buf()
    b_wst = [b_wst1, b_wst1]
    wbf = [AB.alloc(4, 8, 128) for _ in range(2)]
    b_wbf = [Buf() for _ in range(2)]
    NSL = 2
    Pf = [AB.alloc(2, 514) for _ in range(NSL)]
    b_Pf = [[Buf(), Buf()] for _ in range(NSL)]
    cacc = [AFL.alloc(2, 512) for _ in range(NSL)]
    b_cacc = [[Buf(), Buf()] for _ in range(NSL)]
    sgs = [AB.alloc(512) for _ in range(NSL)]
    b_sgs = [Buf() for _ in range(NSL)]
    uts = [AB.alloc(512) for _ in range(NSL)]
    b_uts = [Buf() for _ in range(NSL)]
    wts = [AB.alloc(512) for _ in range(NSL)]
    b_wts = [Buf() for _ in range(NSL)]
    w2ts = [AB.alloc(512) for _ in range(NSL)]
    b_w2ts = [Buf() for _ in range(NSL)]
    b_hal = [PB[4], PB[5]]
    b_ubuf, b_wsp, b_w2sp, b_ybsp, b_kkd, b_ycd = Buf(), Buf(), Buf(), Buf(), Buf(), Buf()
    wcount = [0]
    ucount = [0]

    def preprocess(xT_d):
        chunks = [(512 * k, 512) for k in range(8)] + [(4096, 2)]
        for ci, (c0, n) in enumerate(chunks):
            xst, b_xst = xsts[ci % 2], b_xsts[ci % 2]
            dma("sp", xst[:, :, 0:n], xT_d[:, :, c0:c0 + n], w=[b_xst])
            op("act", lambda e, n=n, xst=xst: e.activation(sq[:, :, 0:n], xst[:, :, 0:n], AF.Square),
               r=[b_xst], w=[b_sq])
            for kt in range(8):
                op("pe", mm(psum[7][:, 0:n], ones, sq[:, kt, 0:n], kt == 0, kt == 7),
                   r=[b_sq, b_tabs], w=[PB[7]])
            op("act", lambda e, n=n: e.activation(rstd[:, 0:n], psum[7][:, 0:n], AF.Sqrt,
                                                  bias=eps_s, scale=1.0 / D),
               r=[PB[7], b_vec], w=[b_rstd])
            op("dve", lambda e, n=n: e.reciprocal(rstd[:, 0:n], rstd[:, 0:n]), r=[b_rstd], w=[b_rstd])
            for kt in range(8):
                op("dve", lambda e, n=n, kt=kt, c0=c0, xst=xst: e.tensor_tensor(
                    hT[:, kt, c0:c0 + n], xst[:, kt, 0:n], rstd[:, 0:n], ALU.mult),
                   r=[b_xst, b_rstd], w=[b_hT])

    def load_w(cts):
        ws = wcount[0] % 2
        wcount[0] += 1
        for i, ct in enumerate(cts):
            dma("sp", wst[ws][:, :, 128 * i:128 * i + 128], win[ct, :, :, :], w=[b_wst[ws]])
        n = len(cts)
        for kt in range(8):
            op("act", lambda e, kt=kt, n=n, ws=ws: e.activation(
                wbf[ws][:, 0:n, kt, :], wst[ws][:, kt, 0:128 * n].rearrange("p (a b) -> p a b", a=n),
                AF.Copy, scale=prew_s[:, kt:kt + 1]),
               r=[b_wst[ws], b_vec], w=[b_wbf[ws]])
        return ws

    def proj_fm(bank, ws, wi, c0):
        for kt in range(8):
            op("pe", mm(psum[bank][:, :], wbf[ws][:, wi, kt, :], hT[:, kt, c0:c0 + 512], kt == 0, kt == 7),
               r=[b_wbf[ws], b_hT], w=[PB[bank]])

    def halo_proj(ws, wi, sl, off, j):
        for kt in range(8):
            op("pe", mm(psum[4 + sl][:, off:off + 2], wbf[ws][:, wi, kt, :],
                        hT[:, kt, 512 * j:512 * j + 514:513], kt == 0, kt == 7),
               r=[b_wbf[ws], b_hT], w=[b_hal[sl]])

    def proj_fm_halo(bank, ws, wi, c0, sl, off, j):
        for kt in range(8):
            op("pe", mm(psum[bank][:, :], wbf[ws][:, wi, kt, :], hT[:, kt, c0:c0 + 512], kt == 0, kt == 7),
               r=[b_wbf[ws], b_hT], w=[PB[bank]])
            op("pe", mm(psum[4 + sl][:, off:off + 2], wbf[ws][:, wi, kt, :],
                        hT[:, kt, 512 * j:512 * j + 514:513], kt == 0, kt == 7),
               r=[b_wbf[ws], b_hT], w=[b_hal[sl]])

    def evac_pf(sl, k, bank, off):
        op("act", lambda e: e.activation(Pf[sl][:, k, 1:513], psum[bank][:, :], AF.Copy),
           r=[PB[bank]], w=[b_Pf[sl][k]])
        op("act", lambda e: e.activation(Pf[sl][:, k, 0:514:513], psum[4 + sl][:, off:off + 2],
                                         AF.Copy), r=[b_hal[sl]], w=[b_Pf[sl][k]])

    def unitA(ws, g, j, tbase, sl):
        bA, bB = 2 * sl, 2 * sl + 1
        c0 = 1 + 512 * j
        proj_fm_halo(bA, ws, 0, c0, sl, 0, j)
        proj_fm_halo(bB, ws, 1, c0, sl, 2, j)
        evac_pf(sl, 0, bA, 0)
        evac_pf(sl, 1, bB, 2)
        cts = (8 + g, 16 + g)
        for tap in range(3):
            for k in range(2):
                ct = cts[k]
                acc = cacc[sl][:, k, :]
                if tap == 0:
                    bank = (bA, bB)[k]
                    op("act", lambda e, acc=acc, bank=bank, ct=ct: e.activation(
                        acc, psum[bank][:, :], AF.Identity, bias=cb_s[:, ct:ct + 1],
                        scale=cw_s[:, 3 * ct + 1:3 * ct + 2]), r=[PB[bank], b_vec], w=[b_cacc[sl][k]])
                else:
                    lo = 0 if tap == 1 else 2
                    wi_ = 3 * ct + (0 if tap == 1 else 2)
                    op("dve", lambda e, acc=acc, k=k, lo=lo, wi_=wi_: e.scalar_tensor_tensor(
                        acc, Pf[sl][:, k, lo:lo + 512], cw_s[:, wi_:wi_ + 1], acc, ALU.mult, ALU.add),
                       r=[b_Pf[sl][k], b_vec], w=[b_cacc[sl][k]])
        op("dve", lambda e: e.tensor_tensor(uts[sl], cacc[sl][:, 0, :], cacc[sl][:, 1, :], ALU.mult),
           r=[b_cacc[sl][0], b_cacc[sl][1]], w=[b_uts[sl]])
        dma("pool", ubuf[128 * g:128 * g + 128, tbase + 512 * j:tbase + 512 * j + 512], uts[sl],
            r=[b_uts[sl]], w=[b_ubuf])

    def unitB(ws, g, j, sl, slA):
        bA, bB = 2 * sl, 2 * sl + 1
        c0 = 1 + 512 * j
        proj_fm_halo(bA, ws, 2, c0, sl, 0, j)
        proj_fm(bB, ws, 3, c0)
        evac_pf(sl, 0, bA, 0)
        op("act", lambda e: e.activation(sgs[sl], psum[bB][:, :], AF.Silu), r=[PB[bB]], w=[b_sgs[sl]])
        ct = g
        acc = cacc[sl][:, 0, :]
        pf = Pf[sl]
        op("act", lambda e: e.activation(acc, psum[bA][:, :], AF.Identity, bias=cb_s[:, ct:ct + 1],
                                         scale=cw_s[:, 3 * ct + 1:3 * ct + 2]),
           r=[PB[bA], b_vec], w=[b_cacc[sl][0]])
        for lo, wi_ in ((0, 3 * ct), (2, 3 * ct + 2)):
            op("dve", lambda e, lo=lo, wi_=wi_: e.scalar_tensor_tensor(
                acc, pf[:, 0, lo:lo + 512], cw_s[:, wi_:wi_ + 1], acc, ALU.mult, ALU.add),
               r=[b_Pf[sl][0], b_vec], w=[b_cacc[sl][0]])
        op("dve", lambda e: e.tensor_tensor(wts[sl], acc, sgs[sl], ALU.mult),
           r=[b_cacc[sl][0], b_sgs[sl]], w=[b_wts[sl]])
        op("dve", lambda e: e.scalar_tensor_tensor(w2ts[sl], uts[slA], skip_s[:, g:g + 1], wts[sl],
                                                   ALU.mult, ALU.mult),
           r=[b_uts[slA], b_wts[sl], b_vec], w=[b_w2ts[sl]])
        cs_ = slice(512 * j, 512 * j + 512)
        dma("pool", wsp[128 * g:128 * g + 128, cs_], wts[sl], r=[b_wts[sl]], w=[b_wsp])
        dma("pool", w2sp[128 * g:128 * g + 128, cs_], w2ts[sl], r=[b_w2ts[sl]], w=[b_w2sp])

    def hyena_pass(own):
        tbase = 0 if own else NT

        def cts_of(g):
            return [8 + g, 16 + g, g, 24 + g] if own else [8 + g, 16 + g]
        nxt = load_w(cts_of(0))
        for g in range(8):
            ws = nxt
            if g + 1 < 8:
                nxt = load_w(cts_of(g + 1))
            for j in range(8):
                sl = ucount[0] % NSL
                ucount[0] += 1
                unitA(ws, g, j, tbase, sl)
                if own:
                    sl2 = ucount[0] % NSL
                    ucount[0] += 1
                    unitB(ws, g, j, sl2, sl)

    def two(fn):
        return [fn() for _ in range(2)]
    vsb = two(lambda: AFL.alloc(512))
    b_vsb = two(Buf)
    vsq = two(lambda: AFL.alloc(512))
    b_vsq = two(Buf)
    st = two(lambda: AFL.alloc(16))
    b_st = two(Buf)
    nrm = two(lambda: AB.alloc(512))
    b_nrm = [[Buf() for _ in range(4)] for _ in range(2)]
    mixed = two(lambda: AFL.alloc(512))
    b_mixed = [[Buf() for _ in range(4)] for _ in range(2)]
    tmpf = two(lambda: AFL.alloc(512))
    b_tmpf = two(Buf)
    ybt = two(lambda: AB.alloc(512))
    b_ybt = two(Buf)
    sgq = two(lambda: AB.alloc(512))
    b_sgq = two(Buf)
    wsT = two(lambda: AB.alloc(128))
    b_wsT = two(Buf)
    wsTf = two(lambda: AFL.alloc(128))
    b_wsTf = two(Buf)
    sgb_bc = two(lambda: AFL.alloc(128))
    b_sgb = two(Buf)
    C2 = two(lambda: AFL.alloc(128))
    b_C2 = two(Buf)
    gcount = [0]

    def gmlp_head_prep(hd):
        hs = hd % 2
        dma("sp", wsTf[hs], sguwT[hd, :, :], w=[b_wsTf[hs]])
        op("act", lambda e: e.activation(wsT[hs], wsTf[hs], AF.Copy), r=[b_wsTf[hs]], w=[b_wsT[hs]])
        dma("sp", sgb_bc[hs], sgub[hd:hd + 1, :].partition_broadcast(128).rearrange("p o n -> p (o n)"),
            w=[b_sgb[hs]])

    def gmlp_tile(ws, hd, j, sl):
        hs = hd % 2
        bU, bG, bV, bM = 4 * sl, 4 * sl + 1, 4 * sl + 2, 4 * sl + 3
        c0 = 1 + 512 * j
        S = st[sl]
        bS = b_st[sl]

        def s0():
            for q in range(4):
                for kt in range(8):
                    op("pe", mm(psum[bV][:, 128 * q:128 * q + 128],
                                hT[:, kt, c0 + 128 * q:c0 + 128 * q + 128], wbf[ws][:, 1, kt, :],
                                kt == 0, kt == 7),
                       r=[b_wbf[ws], b_hT], w=[PB[bV]])
            proj_fm(bU, ws, 0, c0)
            proj_fm(bG, ws, 2, c0)

        def s1():
            op("act", lambda e: e.activation(vsb[sl], psum[bV][:, :], AF.Copy), r=[PB[bV]], w=[b_vsb[sl]])
            op("act", lambda e: e.activation(vsq[sl], psum[bV][:, :], AF.Square), r=[PB[bV]], w=[b_vsq[sl]])
            op("act", lambda e: e.activation(sgq[sl], psum[bG][:, :], AF.Silu), r=[PB[bG]], w=[b_sgq[sl]])

        def s2():
            op("dve", lambda e: e.tensor_reduce(S[:, 0:4], vsb[sl].rearrange("p (a b) -> p a b", a=4),
                                                AX.X, ALU.add), r=[b_vsb[sl]], w=[bS])

        def s3():
            op("dve", lambda e: e.tensor_reduce(S[:, 4:8], vsq[sl].rearrange("p (a b) -> p a b", a=4),
                                                AX.X, ALU.add), r=[b_vsq[sl]], w=[bS])

        def s4():
            op("dve", lambda e: e.tensor_scalar(S[:, 8:12], S[:, 0:4], 1.0 / 128, None, ALU.mult),
               r=[bS], w=[bS])

        def s5():
            op("dve", lambda e: e.tensor_tensor(S[:, 12:16], S[:, 8:12], S[:, 8:12], ALU.mult), r=[bS], w=[bS])

        def s6():
            op("dve", lambda e: e.tensor_scalar(S[:, 4:8], S[:, 4:8], 1.0 / 128, None, ALU.mult),
               r=[bS], w=[bS])

        def s7():
            op("dve", lambda e: e.tensor_tensor(S[:, 12:16], S[:, 4:8], S[:, 12:16], ALU.subtract),
               r=[bS], w=[bS])

        def s8():
            op("act", lambda e: e.activation(S[:, 12:16], S[:, 12:16], AF.Sqrt, bias=eps_s, scale=1.0),
               r=[bS, b_vec], w=[bS])

        def s9():
            op("dve", lambda e: e.reciprocal(S[:, 12:16], S[:, 12:16]), r=[bS], w=[bS])

        def s10():
            for q in range(4):
                op("dve", lambda e, q=q: e.tensor_scalar(
                    nrm[sl][:, 128 * q:128 * q + 128], vsb[sl][:, 128 * q:128 * q + 128],
                    S[:, 8 + q:9 + q], S[:, 12 + q:13 + q], ALU.subtract, ALU.mult),
                   r=[b_vsb[sl], bS], w=[b_nrm[sl][q]])

        def s11():
            for q in range(4):
                op("pe", mm(psum[bM][:, 128 * q:128 * q + 128], nrm[sl][:, 128 * q:128 * q + 128], wsT[hs],
                            True, True), r=[b_nrm[sl][q], b_wsT[hs]], w=[PB[bM]])

        def s12():
            for q in range(4):
                op("dve", lambda e, q=q: e.scalar_tensor_tensor(
                    mixed[sl][:, 128 * q:128 * q + 128], psum[bM][:, 128 * q:128 * q + 128],
                    lnw_s[:, hd:hd + 1], C2[hs], ALU.mult, ALU.add),
                   r=[PB[bM], b_C2[hs], b_vec], w=[b_mixed[sl][q]])

        def s13():
            op("dve", lambda e: e.tensor_tensor(tmpf[sl], psum[bU][:, :], mixed[sl], ALU.mult),
               r=[PB[bU]] + b_mixed[sl], w=[b_tmpf[sl]])

        def s14():
            op("dve", lambda e: e.tensor_tensor(ybt[sl], tmpf[sl], sgq[sl], ALU.mult),
               r=[b_tmpf[sl], b_sgq[sl]], w=[b_ybt[sl]])
            dma("pool", ybsp[128 * hd:128 * hd + 128, 512 * j:512 * j + 512], ybt[sl],
                r=[b_ybt[sl]], w=[b_ybsp])

        return [s0, s1, s2, s3, s4, s5, s6, s7, s8, s9, s10, s11, s12, s13, s14]

    def gmlp_pass():
        nxt = load_w([32, 40, 48])
        gmlp_head_prep(0)
        for hd in range(8):
            ws = nxt
            hs = hd % 2
            op("pe", mm(psum[7][:, 0:128], ones, wsT[hs], True, True), r=[b_wsT[hs], b_tabs], w=[PB[7]])
            op("dve", lambda e, hd=hd, hs=hs: e.scalar_tensor_tensor(
                C2[hs], psum[7][:, 0:128], lnb_s[:, hd:hd + 1], sgb_bc[hs], ALU.mult, ALU.add),
               r=[PB[7], b_sgb[hs], b_vec], w=[b_C2[hs]])
            if hd + 1 < 8:
                nxt = load_w([32 + hd + 1, 40 + hd + 1, 48 + hd + 1])
                gmlp_head_prep(hd + 1)
            for jp in range(4):
                sa = gmlp_tile(ws, hd, 2 * jp, 0)
                sb = gmlp_tile(ws, hd, 2 * jp + 1, 1)
                for k in range(len(sa)):
                    sa[k]()
                    sb[k]()

    KSTOP = int(os.environ.get("KSTOP", "9"))

    def finish():
        P.final_waits("pool")
        with nc.Block() as block:
            @block.tensor
            def _(e):
                P.emit("pe", e)

            @block.scalar
            def _(e):
                P.emit("act", e)

            @block.vector
            def _(e):
                P.emit("dve", e)

            @block.gpsimd
            def _(e):
                P.emit("pool", e)

            @block.sync
            def _(e):
                P.emit("sp", e)
        P.stack.close()
        return nc

    preprocess(xT_o)
    if KSTOP == 0:
        return finish()
    hyena_pass(False)
    if KSTOP == 1:
        return finish()
    preprocess(xT_s)
    hyena_pass(True)
    P.barrier()
    gmlp_pass()
    if KSTOP == 2:
        return finish()
    P.barrier()
    AB.reset()
    AFL.reset()

    w1s = AFL.alloc(64)
    w2s = AFL.alloc(64)
    w3s = AFL.alloc(64)
    wos = AFL.alloc(2048)
    b_fw = Buf()
    dma("sp", w1s[0:33, :], fw1[:, :], w=[b_fw])
    dma("sp", w2s[0:64, :], fw2[:, :], w=[b_fw])
    dma("sp", w3s[0:64, :], fw3[:, :], w=[b_fw])
    dma("sp", wos[0:64, :], fwo[:, :], w=[b_fw])
    wos_b = AB.alloc(2048)
    b_wosb = Buf()
    op("act", lambda e: e.activation(wos_b[0:64, :], wos[0:64, :], AF.Copy), r=[b_fw], w=[b_wosb])
    h3b = AB.alloc(2048)
    zb = AFL.alloc(2048)
    b_zb = Buf()
    tlb = AFL.alloc(2048)
    b_tlb = Buf()
    hbuf = [AFL.alloc(2048) for _ in range(3)]
    b_h = [Buf() for _ in range(3)]
    argbs = [AFL.alloc(512) for _ in range(2)]
    b_args = [Buf(), Buf()]
    argks = [AFL.alloc(512) for _ in range(2)]
    b_argks = [Buf(), Buf()]
    decbs = [AFL.alloc(512) for _ in range(2)]
    b_decs = [Buf(), Buf()]
    kkss = [AB.alloc(2048) for _ in range(2)]
    b_kkss = [[Buf() for _ in range(4)] for _ in range(2)]
    junks = [AFL.alloc(512) for _ in range(2)]
    b_junks = [Buf(), Buf()]
    b_l1list = []

    def b_l1x():
        b_l1list.append(Buf())
        return b_l1list[-1]
    MAGIC = 12582912.0
    mlp_banks = [0, 2, 3]
    out_banks = [4, 5, 6, 7]
    mcount = 0
    ocount = 0
    kcount = 0
    for blk in range(8):
        cs = slice(2048 * blk, 2048 * blk + 2048)
        dma("sp", zb[0:33, :], zT[:, cs], w=[b_zb])
        dma("sp", tlb, tl[0:1, cs].partition_broadcast(128).rearrange("p o n -> p (o n)"), w=[b_tlb])
        srcs = [(zb, b_zb, 33, w1s), (hbuf[0], b_h[0], 64, w2s), (hbuf[1], b_h[1], 64, w3s)]
        for li, (src, bsrc, kk_, wl) in enumerate(srcs):
            for c4 in range(4):
                c_ = slice(512 * c4, 512 * c4 + 512)
                bk = mlp_banks[mcount % 3]
                argb, b_arg = argbs[mcount % 2], b_args[mcount % 2]
                argk, b_argk = argks[mcount % 2], b_argks[mcount % 2]
                mcount += 1
                op("pe", mm(psum[bk][0:64, :], wl[0:kk_, :], src[0:kk_, c_], True, True),
                   r=[b_fw, bsrc], w=[PB[bk]])
                op("dve", lambda e, li=li, bk=bk, argb=argb: e.tensor_scalar(
                    argb[0:64, :], psum[bk][0:64, :], ffb_s[0:64, 2 * li:2 * li + 1], frb_s[0:64, li:li + 1],
                    ALU.mult, ALU.add), r=[PB[bk], b_vec], w=[b_arg])
                op("dve", lambda e, argb=argb, argk=argk: e.tensor_scalar(
                    argk[0:64, :], argb[0:64, :], 1.0 / (2 * math.pi), MAGIC, ALU.mult, ALU.add),
                   r=[b_arg], w=[b_argk])
                op("dve", lambda e, argk=argk: e.tensor_scalar(argk[0:64, :], argk[0:64, :], -MAGIC, 2 * math.pi,
                                                              ALU.add, ALU.mult), r=[b_argk], w=[b_argk])
                op("dve", lambda e, argb=argb, argk=argk: e.tensor_tensor(
                    argb[0:64, :], argb[0:64, :], argk[0:64, :], ALU.subtract), r=[b_arg, b_argk], w=[b_arg])
                hdst = h3b if li == 2 else hbuf[li]
                op("act", lambda e, c_=c_, argb=argb, hdst=hdst: e.activation(
                    hdst[0:64, c_], argb[0:64, :], AF.Sin, bias=zero_s[0:64, :], scale=1.0),
                   r=[b_arg, b_vec], w=[b_h[li]])
        for ct in range(8):
            wcol = (0 if blk < 4 else 1024) + 128 * ct
            kks, b_kks = kkss[kcount % 2], b_kkss[kcount % 2]
            kcount += 1
            for c4 in range(4):
                c_ = slice(512 * c4, 512 * c4 + 512)
                bk = out_banks[ocount % 4]
                decb, b_dec = decbs[ocount % 2], b_decs[ocount % 2]
                junk, b_junk = junks[ocount % 2], b_junks[ocount % 2]
                ocount += 1
                op("pe", mm(psum[bk][:, :], wos_b[0:64, wcol:wcol + 128], h3b[0:64, c_], True, True),
                   r=[b_wosb, b_h[2]], w=[PB[bk]])
                op("act", lambda e, ct=ct, c_=c_, decb=decb: e.activation(decb, tlb[:, c_], AF.Exp,
                                                                         scale=ndel_s[:, ct:ct + 1]),
                   r=[b_tlb, b_vec], w=[b_dec])
                op("dve", lambda e, c_=c_, bk=bk, decb=decb, kks=kks: e.tensor_tensor(
                    kks[:, c_], psum[bk][:, :], decb, ALU.mult), r=[PB[bk], b_dec], w=[b_kks[c4]])
                op("dve", lambda e, ct=ct, blk=blk, c4=c4, c_=c_, kks=kks: e.tensor_reduce(
                    l1acc[:, ct, 4 * blk + c4:4 * blk + c4 + 1], kks[:, c_], AX.X, ALU.add,
                    apply_absolute_value=True),
                   r=[b_kks[c4], b_l1], w=[b_l1x()])
            dma("pool", kkd[128 * ct:128 * ct + 128, cs], kks, r=b_kks, w=[b_kkd])
    op("dve", lambda e: e.tensor_reduce(inv_s, l1acc, AX.X, ALU.add), r=[b_l1] + b_l1list, w=[b_vec])
    op("dve", lambda e: e.tensor_scalar(inv_s, inv_s, EPS, float(NF), ALU.add, ALU.mult), r=[b_vec], w=[b_vec])
    op("dve", lambda e: e.reciprocal(inv_s, inv_s), r=[b_vec], w=[b_vec])
    dump("inv", inv_s, b_vec)
    dump("l1acc", l1acc, b_l1)
    if KSTOP == 3:
        return finish()
    P.barrier()
    AB.reset()
    AFL.reset()

    wo_b = AB.alloc(16, 1024)
    b_wo = Buf()
    wo_st = AFL.alloc(1024)
    b_wost = Buf()
    for ct in range(16):
        dma("sp", wo_st, wout[ct, :, :], w=[b_wost])
        op("act", lambda e, ct=ct: e.activation(wo_b[:, ct, :], wo_st, AF.Copy), r=[b_wost], w=[b_wo])
    AB.mark()
    AFL.mark()
    KKs = [AB.alloc(16, 128) for _ in range(2)]
    b_KKs = [Buf(), Buf()]
    Zs = [AB.alloc(16, 128) for _ in range(2)]
    b_Zs = [Buf(), Buf()]
    Y16s = [AB.alloc(16, 128) for _ in range(2)]
    b_Y16s = [Buf(), Buf()]
    PQs = [AB.alloc(2, 2, 130) for _ in range(4)]
    b_PQs = [[Buf(), Buf(), Buf()] for _ in range(4)]
    PQ2s = [AB.alloc(4, 2, 65) for _ in range(4)]
    b_PQ2s = [[Buf(), Buf(), Buf()] for _ in range(4)]
    PQ3s = [AB.alloc(2, 2, 256) for _ in range(4)]
    b_PQ3s = [[Buf(), Buf(), Buf()] for _ in range(4)]
    Kcs = [AB.alloc(528) for _ in range(4)]
    b_Kcs = [[Buf(), Buf(), Buf()] for _ in range(4)]
    Cbs = [AB.alloc(512) for _ in range(4)]
    b_Cbs = [Buf() for _ in range(4)]
    kkv = kkd.ap().rearrange("c (a p) -> a c p", p=128)
    ubv = ubuf.ap().rearrange("c (a p) -> a c p", p=128)
    ycv = ycd.ap().rearrange("c (a p) -> a c p", p=128)

    def twiddle_f(sl, dst, bdst):
        a = psum[2 * sl][:, 0:260]
        pb = PB[2 * sl]
        a3 = a.rearrange("p (c r k) -> p c r k", c=2, r=2)
        d0 = dst[:, 0, :, :].rearrange("p c k -> p (c k)")
        d1 = dst[:, 1, :, :].rearrange("p c (r k) -> p c r k", r=2)
        op("dve", lambda e: e.tensor_tensor(d0, a, TT_P65, ALU.mult), r=[pb, b_tabs], w=[bdst[0]])
        op("dve", lambda e: e.tensor_tensor(d1[:, :, 0, :], a3[:, :, 1, :],
                                            TT_n65.rearrange("p (c k) -> p c k", c=2), ALU.mult),
           r=[pb, b_tabs], w=[bdst[1]])
        op("dve", lambda e: e.tensor_tensor(d1[:, :, 1, :], a3[:, :, 0, :],
                                            TT_p65.rearrange("p (c k) -> p c k", c=2), ALU.mult),
           r=[pb, b_tabs], w=[bdst[2]])

    def twiddle_i(sl, dst, bdst):
        a = Cbs[sl][0:65, :]
        pb = b_Cbs[sl]
        op("act", lambda e: e.activation(a, psum[2 * sl + 1][0:65, :], AF.Copy), r=[PB[2 * sl + 1]], w=[pb])
        a3 = a.rearrange("p (c r k) -> p c r k", c=2, r=2)
        d0 = dst[0:65, 0, :, :].rearrange("p c k -> p (c k)")
        d1 = dst[0:65, 1, :, :].rearrange("p c (r k) -> p c r k", r=2)
        op("dve", lambda e: e.tensor_tensor(d0, a, TTb_P[0:65, :], ALU.mult), r=[pb, b_tabs], w=[bdst[0]])
        op("dve", lambda e: e.tensor_tensor(d1[:, :, 0, :], a3[:, :, 1, :],
                                            TTb_p[0:65, :].rearrange("p (c k) -> p c k", c=2), ALU.mult),
           r=[pb, b_tabs], w=[bdst[1]])
        op("dve", lambda e: e.tensor_tensor(d1[:, :, 1, :], a3[:, :, 0, :],
                                            TTb_n[0:65, :].rearrange("p (c k) -> p c k", c=2), ALU.mult),
           r=[pb, b_tabs], w=[bdst[2]])

    def stage2b(sl):
        src, bsrc = PQs[sl], b_PQs[sl]
        X = psum[2 * sl + 1]

        def part(pq, r):
            return src[:, pq, :, 65 * r:65 * r + 65]
        seq_r = [(nWi, part(0, 1)), (nWi, part(1, 1)), (Wr, part(0, 0)), (Wr, part(1, 0))]
        seq_i = [(Wr, part(0, 1)), (Wr, part(1, 1)), (Wi, part(0, 0)), (Wi, part(1, 0))]
        for half, seq in ((0, seq_r), (1, seq_i)):
            o = X[:, 130 * half:130 * half + 130].rearrange("p (c k) -> p c k", c=2)
            for i, (wm, rr) in enumerate(seq):
                op("pe", mm(o, wm, rr, i == 0, i == 3), r=bsrc + [b_tabs], w=[PB[2 * sl + 1]])

    def chunk_stages(gs, sl, cc0):
        KK, Z, Y16 = KKs[gs], Zs[gs], Y16s[gs]
        A = psum[2 * sl]
        X = psum[2 * sl + 1]
        C = X
        Yp = A
        pA, pX = PB[2 * sl], PB[2 * sl + 1]
        pC, pY = pX, pA
        Kc, bKc = Kcs[sl], b_Kcs[sl]
        PQ2, bPQ2 = PQ2s[sl], b_PQ2s[sl]
        PQ3, bPQ3 = PQ3s[sl], b_PQ3s[sl]

        def s1f():
            for c in range(2):
                op("pe", mm(A[:, 130 * c:130 * c + 130], KK[:, cc0 + c, :], F1a65, True, True),
                   r=[b_KKs[gs], b_tabs], w=[pA])

        def twf():
            twiddle_f(sl, PQs[sl], b_PQs[sl])

        def s2():
            stage2b(sl)

        def evk():
            op("act", lambda e: e.activation(Kc[:, 0:130], X[:, 0:130], AF.Copy), r=[pX], w=[bKc[0]])
            op("act", lambda e: e.activation(Kc[:, 130:390], X[:, 0:260], AF.Copy), r=[pX], w=[bKc[1]])
            op("act", lambda e: e.activation(Kc[:, 390:520], X[:, 130:260], AF.Copy, scale=-1.0),
               r=[pX], w=[bKc[2]])

        def s1d():
            for c in range(2):
                op("pe", mm(A[:, 130 * c:130 * c + 130], Z[0:64, cc0 + c, :], F1d65[0:64, :], True, True),
                   r=[b_Zs[gs], b_tabs], w=[pA])

        def kmul():
            fl = PQ2.rearrange("p a c k -> p (a c k)")
            op("dve", lambda e: e.tensor_tensor(fl[:, 0:260], X[:, 0:260], Kc[:, 0:260], ALU.mult),
               r=[pX, bKc[0], bKc[1]], w=[bPQ2[0]])
            op("dve", lambda e: e.tensor_tensor(fl[:, 260:390], X[:, 130:260], Kc[:, 390:520], ALU.mult),
               r=[pX, bKc[2]], w=[bPQ2[1]])
            op("dve", lambda e: e.tensor_tensor(fl[:, 390:520], X[:, 0:130], Kc[:, 260:390], ALU.mult),
               r=[pX, bKc[1]], w=[bPQ2[2]])

        def s3():
            for c in range(2):
                o = C[0:65, 256 * c:256 * c + 256]
                seq = [(0, G1a), (2, G1a), (1, G1b), (3, G1b)]
                for i, (pi_, tab) in enumerate(seq):
                    op("pe", mm(o, PQ2[:, pi_, c, :], tab, i == 0, i == 3), r=bPQ2 + [b_tabs], w=[pC])

        def itw():
            twiddle_i(sl, PQ3, bPQ3)

        def s4():
            def part3(pq, r):
                return PQ3[0:65, pq, :, 128 * r:128 * r + 128]
            seq = [(Vr4w[0:65, :], part3(0, 0)), (Vr4w[0:65, :], part3(1, 0)),
                   (nVi4w[0:65, :], part3(0, 1)), (nVi4w[0:65, :], part3(1, 1))]
            for i, (wm, rr) in enumerate(seq):
                op("pe", mm(Yp[0:32, 0:256].rearrange("p (c k) -> p c k", c=2), wm, rr, i == 0, i == 3),
                   r=bPQ3 + [b_tabs], w=[pY])

        def evy():
            op("act", lambda e: e.activation(Y16[0:32, cc0:cc0 + 2, :].rearrange("p c k -> p (c k)"),
                                             Yp[0:32, 0:256], AF.Copy), r=[pY], w=[b_Y16s[gs]])

        def s2_s1d():
            s2()
            s1d()

        return [s1f, twf, s2_s1d, evk, twf, s2, kmul, s3, itw, s4, evy]

    for grp in range(64):
        gs = grp % 2
        ch0 = 16 * grp
        dma("sp", KKs[gs], kkv[:, ch0:ch0 + 16, :], r=[b_kkd], w=[b_KKs[gs]])
        dma("sp", Zs[gs][0:64, :, :], ubv[:, ch0:ch0 + 16, :], r=[b_ubuf], w=[b_Zs[gs]])
        for quad in range(2):
            sts = [chunk_stages(gs, sl, 8 * quad + 2 * sl) for sl in range(4)]
            for k in range(len(sts[0])):
                for sl in range(4):
                    sts[sl][k]()
        dma("pool", ycv[:, ch0:ch0 + 16, :], Y16s[gs][0:32, :, :], r=[b_Y16s[gs]], w=[b_ycd])
    if KSTOP == 4:
        return finish()
    P.barrier()
    AB.reset()
    AFL.reset()

    pw_bc = AFL.alloc(1024)
    b_pw = Buf()
    dma("sp", pw_bc, postw[0:1, :].partition_broadcast(128).rearrange("p o n -> p (o n)"), w=[b_pw])
    yc_t = two(lambda: AB.alloc(8, 512))
    b_yct = two(Buf)
    w_t = two(lambda: AB.alloc(8, 512))
    b_w_t = two(Buf)
    w2_t = two(lambda: AB.alloc(8, 512))
    b_w2_t = two(Buf)
    ya_t = two(lambda: AB.alloc(8, 512))
    b_ya = [[Buf() for _ in range(8)] for _ in range(2)]
    yb_t = two(lambda: AB.alloc(8, 512))
    b_yb = two(Buf)
    N3 = 3
    xt_ = [AFL.alloc(1024) for _ in range(N3)]
    b_xt = [Buf() for _ in range(N3)]
    ot = [AFL.alloc(1024) for _ in range(N3)]
    b_ot = [[Buf(), Buf()] for _ in range(N3)]
    sq2 = [AFL.alloc(1024) for _ in range(N3)]
    b_sq2 = [Buf() for _ in range(N3)]
    ss = [AFL.alloc(4) for _ in range(N3)]
    b_ss = [Buf() for _ in range(N3)]

    def v3(d):
        return d.ap().rearrange("(g c) t -> c g t", c=128)

    scount = 0
    for j in range(8):
        jp = j % 2
        ts_ = slice(512 * j, 512 * j + 512)
        dma("sp", yc_t[jp], v3(ycd)[:, :, ts_], r=[b_ycd], w=[b_yct[jp]])
        dma("sp", w_t[jp], v3(wsp)[:, :, ts_], r=[b_wsp], w=[b_w_t[jp]])
        dma("sp", w2_t[jp], v3(w2sp)[:, :, ts_], r=[b_w2sp], w=[b_w2_t[jp]])
        dma("sp", yb_t[jp], v3(ybsp)[:, :, ts_], r=[b_ybsp], w=[b_yb[jp]])
        for g in range(8):
            op("dve", lambda e, g=g, jp=jp: e.scalar_tensor_tensor(
                ya_t[jp][:, g, :], yc_t[jp][:, g, :], inv_s[:, g:g + 1], w_t[jp][:, g, :], ALU.mult, ALU.mult),
               r=[b_yct[jp], b_w_t[jp], b_vec], w=[b_ya[jp][g]])
        for g in range(8):
            op("dve", lambda e, g=g, jp=jp: e.tensor_tensor(ya_t[jp][:, g, :], ya_t[jp][:, g, :],
                                                           w2_t[jp][:, g, :], ALU.add),
               r=[b_w2_t[jp]], w=[b_ya[jp][g]])
        for q in range(4):
            tix = 4 * j + q
            k3 = scount % N3
            scount += 1
            b0 = 2 * k3
            dma("sp", xt_[k3], xn[tix, :, :], w=[b_xt[k3]])
            for ct in range(16):
                if ct < 8:
                    src, bsrc = ya_t[jp][:, ct, 128 * q:128 * q + 128], b_ya[jp][ct]
                else:
                    src, bsrc = yb_t[jp][:, ct - 8, 128 * q:128 * q + 128], b_yb[jp]
                for hf in range(2):
                    op("pe", mm(psum[b0 + hf][:, :], src, wo_b[:, ct, 512 * hf:512 * hf + 512], ct == 0, ct == 15),
                       r=[bsrc, b_wo], w=[PB[b0 + hf]])
            op("dve", lambda e, k3=k3: e.memset(ss[k3], 0.0), w=[b_ss[k3]])
            for hf in range(2):
                op("act", lambda e, hf=hf, k3=k3, b0=b0: e.activation(
                    sq2[k3][:, 512 * hf:512 * hf + 512], psum[b0 + hf][:, :],
                    AF.Square, accum_out=ss[k3][:, hf:hf + 1]),
                   r=[PB[b0 + hf]], w=[b_sq2[k3], b_ss[k3]])
            op("dve", lambda e, k3=k3: e.tensor_tensor(ss[k3][:, 2:3], ss[k3][:, 0:1], ss[k3][:, 1:2], ALU.add),
               r=[b_ss[k3]], w=[b_ss[k3]])
            op("act", lambda e, k3=k3: e.activation(ss[k3][:, 3:4], ss[k3][:, 2:3], AF.Sqrt, bias=eps_s,
                                                    scale=1.0 / D), r=[b_ss[k3], b_vec], w=[b_ss[k3]])
            op("dve", lambda e, k3=k3: e.reciprocal(ss[k3][:, 3:4], ss[k3][:, 3:4]), r=[b_ss[k3]], w=[b_ss[k3]])
            for hf in range(2):
                op("dve", lambda e, hf=hf, k3=k3, b0=b0: e.scalar_tensor_tensor(
                    ot[k3][:, 512 * hf:512 * hf + 512], psum[b0 + hf][:, :], ss[k3][:, 3:4],
                    pw_bc[:, 512 * hf:512 * hf + 512], ALU.mult, ALU.mult),
                   r=[PB[b0 + hf], b_ss[k3], b_pw], w=[b_ot[k3][hf]])
            for hf in range(2):
                op("dve", lambda e, hf=hf, k3=k3: e.tensor_tensor(
                    ot[k3][:, 512 * hf:512 * hf + 512], ot[k3][:, 512 * hf:512 * hf + 512],
                    xt_[k3][:, 512 * hf:512 * hf + 512], ALU.add),
                   r=[b_xt[k3]], w=[b_ot[k3][hf]])
            dma("pool", out[tix, :, :], ot[k3], r=b_ot[k3], w=[Buf()])
    P.final_waits("pool")

    with nc.Block() as block:
        @block.tensor
        def _(e):
            P.emit("pe", e)

        @block.scalar
        def _(e):
            P.emit("act", e)

        @block.vector
        def _(e):
            P.emit("dve", e)

        @block.gpsimd
        def _(e):
            P.emit("pool", e)

        @block.sync
        def _(e):
            P.emit("sp", e)
    P.stack.close()
    return nc


def _tables(h):
    n = np.arange(128)
    ang = 2 * np.pi * np.outer(n, n) / 128.0
    wr, wi = np.cos(ang), -np.sin(ang)
    tb = np.zeros((128, 3072), np.float64)
    tb[:, 0:128] = 1.0
    tb[:, 128:256], tb[:, 256:384] = wr, wi
    tb[:, 384:512], tb[:, 512:640], tb[:, 640:768] = wr, wi, -wi
    tb[:, 768:896], tb[:, 896:1024] = wr, -wi
    tb[:, 1024:1152], tb[:, 1152:1280] = wi, wr
    own = np.arange(32 * h, 32 * h + 32)
    tb[:, 1280:1312] = wr[:, own]
    tb[:, 1312:1344] = wi[:, own]
    perm = np.array([(32 * h + a) if a < 32 else (32 * (1 - h) + a - 32) for a in range(64)])
    tb[0:64, 1408:1536] = wr[perm, :]
    tb[0:64, 1536:1664] = wi[perm, :]
    tb[:, 1664:1729], tb[:, 1729:1794] = wr[:, 0:65], wi[:, 0:65]
    tb[0:64, 1794:1859], tb[0:64, 1859:1924] = wr[perm, 0:65], wi[perm, 0:65]
    wgt = np.full((65, 1), 2.0)
    wgt[0, 0] = 1.0
    wgt[64, 0] = 1.0
    tb[0:65, 1924:1956] = wgt * wr[0:65][:, own]
    tb[0:65, 1956:1988] = wgt * wi[0:65][:, own]
    ang2 = 2 * np.pi * np.outer(n, n) / float(NF)
    tr, ti = np.cos(ang2), -np.sin(ang2)
    t65, i65 = tr[:, 0:65], ti[:, 0:65]
    tb[:, 2048:3072] = np.concatenate([tr, tr, tr, tr, -ti, -ti, ti, ti], axis=1)
    tf = np.concatenate([tr, tr, tr, tr, -ti, -ti, ti, ti, t65, t65, t65, t65, -i65, -i65, i65, i65,
                         np.zeros((128, 8))], axis=1)
    import ml_dtypes
    return tb.astype(np.float32).astype(ml_dtypes.bfloat16), tf.astype(np.float32)


def _consts():
    bands = 16
    t = np.linspace(0.0, 1.0, L, dtype=np.float32)[:, None]
    w = (2.0 * np.float32(math.pi) * np.arange(L, dtype=np.float32)[:, None] / np.float32(L)).astype(np.float32)
    f = np.linspace(1e-4, bands - 1, bands, dtype=np.float32)[None, :]
    z = np.concatenate([t, np.cos(f * w), -np.sin(f * w)], axis=-1).astype(np.float32)
    order = np.concatenate([np.arange(L), [0], np.arange(L - 1, 0, -1)])
    zT = np.ascontiguousarray(z[order].T)
    tl = t[order, 0].copy()
    tl[L] = 1.0e4
    deltas = np.abs(np.linspace(math.log(1e-2) / 1.5, math.log(1e-2) / 0.3, D, dtype=np.float32))
    ndelta = np.ascontiguousarray((-deltas).reshape(8, 128).T)
    return zT.astype(np.float32), tl.reshape(1, NF).astype(np.float32), ndelta.astype(np.float32)


def kernel(x, pre_norm_w, w_in, conv_w, conv_b, filt_w1, filt_b1, filt_freq1, filt_w2, filt_b2,
           filt_freq2, filt_w3, filt_b3, filt_freq3, filt_w_out, hyena_skip, sgu_norm_w, sgu_norm_b,
           sgu_w, sgu_b, w_out, post_norm_w):
    f = lambda a: np.ascontiguousarray(np.asarray(a, dtype=np.float32))
    x = f(x)
    nc = build_program()
    zT, tl, ndelta = _consts()
    pk = lambda v: f(np.asarray(v).reshape(-1, 128).T)
    common = {
        "win": f(np.asarray(w_in).reshape(8, 128, 56, 128).transpose(2, 1, 0, 3)),
        "wout": f(np.asarray(w_out).reshape(16, 128, 1024)),
        "prew": pk(pre_norm_w),
        "cw": f(np.asarray(conv_w).reshape(3, 24, 128).transpose(2, 1, 0).reshape(128, 72)),
        "cb": pk(conv_b), "skip": pk(hyena_skip), "lnw": pk(sgu_norm_w), "lnb": pk(sgu_norm_b),
        "sgub": f(sgu_b), "sguwT": f(np.asarray(sgu_w).transpose(0, 2, 1)),
        "postw": f(np.asarray(post_norm_w).reshape(1, 1024)),
        "fw1": f(filt_w1), "fw2": f(filt_w2), "fw3": f(filt_w3), "fwo": f(filt_w_out),
        "ffb": f(np.stack([filt_freq1, filt_b1, filt_freq2, filt_b2, filt_freq3, filt_b3], axis=1)),
        "zT": zT, "tl": tl, "ndelta": ndelta,
    }
    in_maps = []
    for i in range(8):
        b, h = i // 2, i % 2
        xp = np.zeros((L + 2, D), np.float32)
        xp[1:L + 1] = x[b]

        def xt(lo):
            blk = xp[lo:lo + 4098]
            return f(blk.T.reshape(8, 128, 4098).transpose(1, 0, 2))
        tb, tf = _tables(h)
        m = dict(common)
        m["xT_s"] = xt(NT * h)
        m["xT_o"] = xt(NT * (1 - h))
        m["xn"] = f(x[b, NT * h:NT * h + NT].reshape(32, 128, 1024))
        m["tabs_b"] = tb
        m["tabs_f"] = tf
        in_maps.append(m)
    if os.environ.get("KRAW", "0") == "1":
        return run_bass_kernel_spmd(nc, in_maps, core_ids=list(range(8)))
    res = run_bass_kernel_spmd(nc, in_maps, core_ids=list(range(8)))
    outp = np.zeros((4, L, D), np.float32)
    for i in range(8):
        b, h = i // 2, i % 2
        outp[b, NT * h:NT * h + NT] = np.asarray(res.results[i]["out"]).reshape(NT, D)
    return outp
```
